# Optimizing a Trainium2 kernel written in Bass

```python
import math
import jax, jax.numpy as jnp
from jax import lax
import numpy as np

D_MODEL = 2048
BATCH = 4
SEQ = 4096
DEPTH = 1

EPS = 1e-6
MLA_HEADS = 8
QK_NOPE_DIM = 128
QK_ROPE_DIM = 64
V_HEAD_DIM = 128
QK_HEAD_DIM = QK_NOPE_DIM + QK_ROPE_DIM
Q_LORA_RANK = 512
KV_LORA_RANK = 512
MLA_WIDTH = MLA_HEADS * V_HEAD_DIM
CONV_GROUPS = 8
CONV_GROUP_DIM = 128
CONV_WIDTH = CONV_GROUPS * CONV_GROUP_DIM
MIX_WIDTH = MLA_WIDTH + CONV_WIDTH
CONV_KERNEL = 3
ROPE_THETA = 10000.0
Q_BLOCK = 128
IN_SPLITS = (Q_LORA_RANK, KV_LORA_RANK, QK_ROPE_DIM, CONV_WIDTH, CONV_WIDTH, CONV_WIDTH)
IN_WIDTH = sum(IN_SPLITS)
D_FF = 5632

kernel_name = "hymba_mla_shortconv_convffn_encoder_block"


def rmsnorm(x, g):
    xf = x.astype(jnp.float32)
    y = xf * lax.rsqrt(jnp.mean(xf * xf, axis=-1, keepdims=True) + EPS)
    return (y * g.astype(jnp.float32)).astype(x.dtype)


def conv3_centred(h, w):
    hp = jnp.pad(h, ((0, 0), (1, 1), (0, 0)))
    return hp[:, :-2] * w[0] + hp[:, 1:-1] * w[1] + hp[:, 2:] * w[2]


def rope_tables(seq, dim, dtype):
    pos = jnp.arange(seq, dtype=jnp.float32)
    inv_freq = 1.0 / (ROPE_THETA ** (jnp.arange(0, dim, 2, dtype=jnp.float32) / dim))
    ang = pos[:, None] * inv_freq[None, :]
    return jnp.cos(ang).astype(dtype), jnp.sin(ang).astype(dtype)


def apply_rope(x, cos, sin):
    half = x.shape[-1] // 2
    x1, x2 = x[..., :half], x[..., half:]
    c = cos[None, :, None, :]
    s = sin[None, :, None, :]
    return jnp.concatenate([x1 * c - x2 * s, x1 * s + x2 * c], axis=-1)


def dense_bidirectional_attention(q, k, v):
    b, s, h, dqk = q.shape
    dv = v.shape[-1]
    nb = s // Q_BLOCK
    scale = 1.0 / math.sqrt(dqk)
    qb = q.reshape(b, nb, Q_BLOCK, h, dqk).transpose(1, 0, 2, 3, 4)

    def one_block(q_blk):
        scores = jnp.einsum('bqhd,bkhd->bhqk', q_blk, k).astype(jnp.float32) * scale
        probs = jax.nn.softmax(scores, axis=-1).astype(v.dtype)
        return jnp.einsum('bhqk,bkhd->bqhd', probs, v)

    out = lax.map(one_block, qb)
    return out.transpose(1, 0, 2, 3, 4).reshape(b, s, h, dv)


def setup_inputs(seed: int = 0) -> dict:
    key = jax.random.key(seed)
    ks = jax.random.split(key, 20)
    f32 = jnp.float32

    def w(k, shape, fan_in):
        return jax.random.normal(k, shape, f32) * (fan_in ** -0.5)

    def gain(k, shape):
        return 1.0 + 0.02 * jax.random.normal(k, shape, f32)

    L = DEPTH
    return {
        "x": jax.random.normal(ks[0], (BATCH, SEQ, D_MODEL), f32),
        "attn_norm_g": gain(ks[1], (L, D_MODEL)),
        "w_in": w(ks[2], (L, D_MODEL, IN_WIDTH), D_MODEL),
        "q_a_norm_g": gain(ks[3], (L, Q_LORA_RANK)),
        "kv_a_norm_g": gain(ks[4], (L, KV_LORA_RANK)),
        "w_q_b": w(ks[5], (L, Q_LORA_RANK, MLA_HEADS * QK_HEAD_DIM), Q_LORA_RANK),
        "w_kv_b": w(ks[6], (L, KV_LORA_RANK, MLA_HEADS * (QK_NOPE_DIM + V_HEAD_DIM)), KV_LORA_RANK),
        "sc_conv_w": w(ks[7], (L, CONV_KERNEL, CONV_WIDTH), CONV_KERNEL),
        "out_norm_attn_g": gain(ks[8], (L, MLA_WIDTH)),
        "out_norm_conv_g": gain(ks[9], (L, CONV_WIDTH)),
        "w_o": w(ks[10], (L, MIX_WIDTH, D_MODEL), MIX_WIDTH),
        "ffn_norm_g": gain(ks[11], (L, D_MODEL)),
        "w_ffn_up": w(ks[12], (L, D_MODEL, 2 * D_FF), D_MODEL),
        "ffn_conv_w": w(ks[13], (L, CONV_KERNEL, 2 * D_FF), CONV_KERNEL),
        "ffn_conv_b": 0.01 * jax.random.normal(ks[14], (L, 2 * D_FF), f32),
        "w_ffn_down": w(ks[15], (L, D_FF, D_MODEL), D_FF),
        "final_norm_g": gain(ks[16], (D_MODEL,)),
    }


def reference(x, attn_norm_g, w_in, q_a_norm_g, kv_a_norm_g, w_q_b, w_kv_b,
              sc_conv_w, out_norm_attn_g, out_norm_conv_g, w_o, ffn_norm_g,
              w_ffn_up, ffn_conv_w, ffn_conv_b, w_ffn_down, final_norm_g):
    b, s, _ = x.shape
    cos, sin = rope_tables(s, QK_ROPE_DIM, x.dtype)
    split_points = list(np.cumsum(IN_SPLITS)[:-1])

    for l in range(DEPTH):
        h = rmsnorm(x, attn_norm_g[l])
        z = h @ w_in[l]
        c_q, c_kv, k_rope, gate_b, gate_c, sc_h = jnp.split(z, split_points, axis=-1)

        q = (rmsnorm(c_q, q_a_norm_g[l]) @ w_q_b[l]).reshape(b, s, MLA_HEADS, QK_HEAD_DIM)
        q = jnp.concatenate([q[..., :QK_NOPE_DIM], apply_rope(q[..., QK_NOPE_DIM:], cos, sin)], axis=-1)
        kv = (rmsnorm(c_kv, kv_a_norm_g[l]) @ w_kv_b[l]).reshape(b, s, MLA_HEADS, QK_NOPE_DIM + V_HEAD_DIM)
        k_nope, v = kv[..., :QK_NOPE_DIM], kv[..., QK_NOPE_DIM:]
        k_pe = apply_rope(k_rope[:, :, None, :], cos, sin)
        k = jnp.concatenate([k_nope, jnp.broadcast_to(k_pe, (b, s, MLA_HEADS, QK_ROPE_DIM))], axis=-1)
        attn = dense_bidirectional_attention(q, k, v).reshape(b, s, MLA_WIDTH)

        y_conv = gate_b * conv3_centred(gate_c * sc_h, sc_conv_w[l])

        merged = jnp.concatenate([rmsnorm(attn, out_norm_attn_g[l]),
                                  rmsnorm(y_conv, out_norm_conv_g[l])], axis=-1)
        x = x + merged @ w_o[l]

        h = rmsnorm(x, ffn_norm_g[l])
        u = conv3_centred(h @ w_ffn_up[l], ffn_conv_w[l]) + ffn_conv_b[l]
        g, val = u[..., :D_FF], u[..., D_FF:]
        x = x + (jax.nn.silu(g) * val) @ w_ffn_down[l]

    return rmsnorm(x, final_norm_g)
```

```python
import math
import numpy as np
import concourse.bass as bass
import concourse.mybir as mybir
from concourse.bass_utils import run_bass_kernel_spmd

F32 = mybir.dt.float32
BF16 = mybir.dt.bfloat16
AF = mybir.ActivationFunctionType
ALU = mybir.AluOpType

EPS = 1e-6
D = 2048
KC = 16
SEQ = 4096
NOWN = 1024
E = 1028
GROUPS = [(0, 343), (343, 343), (686, 342)]
TILES = [(128 * t, 128) for t in range(8)] + [(1024, 4)]
NH = 8
DFF = 5632
NSUP = 22
SCALE = 1.0 / math.sqrt(192.0)

C_GQ, C_GKV, C_GA, C_GB, C_SCW, C_FCW, C_FCB, NCT = 0, 4, 8, 16, 24, 48, 312, 400

import os
PIPE = os.environ.get('K_PIPE', '1') == '1'
ZPAD = os.environ.get('K_ZPAD', '1') == '1'
SAME_SYNC = {'pe': False, 'act': True, 'dve': True, 'pool': True, 'sp': False}


def tiles_of(c0, n):
    return [t for t, (s, r) in enumerate(TILES) if s < c0 + n and s + r > c0]


class Prog:
    def __init__(self, nc):
        self.nc = nc
        self.engs = {'pe': nc.tensor, 'act': nc.scalar, 'dve': nc.vector,
                     'pool': nc.gpsimd, 'sp': nc.sync}
        self.streams = {e: [] for e in self.engs}
        self.sems = {}
        self.cnt = {}
        self.waited = {e: {} for e in self.engs}
        self.lastw = {}
        self.readers = {}
        self.fam_live = {}
        self.dead = []
        self.inherit = {}
        self.noalias = {'ps'}

    @staticmethod
    def fam_of(key):
        return key if isinstance(key, str) else key[0]

    def on_alloc(self, fam, lo, hi):
        inh = self.inherit.setdefault(fam, {})

        def merge(k, v):
            if inh.get(k, 0) < v:
                inh[k] = v
        hit = [d for d in self.dead if d[0] < hi and lo < d[1]]
        if hit:
            fams = {d[2] for d in hit}
            for k, tok in self.lastw.items():
                if self.fam_of(k) in fams:
                    merge(*tok)
            for k, rd in self.readers.items():
                if self.fam_of(k) in fams:
                    for sk, v in rd.items():
                        merge(sk, v)
            for f in fams:
                for sk, v in list(self.inherit.get(f, {}).items()):
                    merge(sk, v)
            self.dead = [d for d in self.dead if not (lo <= d[0] and d[1] <= hi)]
        self.fam_live[fam] = self.fam_live.get(fam, 0) + 1

    def on_release(self, entries):
        self.dead.extend(entries)

    def _sem(self, key):
        if key not in self.sems:
            self.sems[key] = self.nc.alloc_semaphore("s%d" % len(self.sems))
            self.cnt[key] = 0
        return self.sems[key]

    def _deps(self, eng, reads, writes):
        deps = {}

        def add(k, v):
            if deps.get(k, 0) < v:
                deps[k] = v
        for r in reads:
            if r in self.lastw:
                add(*self.lastw[r])
        for w in writes:
            if w in self.lastw:
                add(*self.lastw[w])
            for k, v in self.readers.get(w, {}).items():
                add(k, v)
        for key in list(reads) + list(writes):
            fam = self.fam_of(key)
            assert fam in self.fam_live or fam in self.noalias, ("unbound region family", key)
            for k, v in self.inherit.get(fam, {}).items():
                add(k, v)
        waits = []
        for k, v in deps.items():
            if k == eng and not SAME_SYNC[eng]:
                continue
            if self.waited[eng].get(k, 0) >= v:
                continue
            self.waited[eng][k] = v
            waits.append((k, v))
        return waits

    def _commit(self, tok, reads, writes):
        for r in reads:
            d = self.readers.setdefault(r, {})
            if d.get(tok[0], 0) < tok[1]:
                d[tok[0]] = tok[1]
        for w in writes:
            self.lastw[w] = tok
            self.readers[w] = {}

    def op(self, eng, fn, reads=(), writes=(), signal=True):
        waits = self._deps(eng, reads, writes)
        self._sem(eng)
        if signal:
            self.cnt[eng] += 1
            tok = (eng, self.cnt[eng])
        else:
            tok = (eng, self.cnt[eng] + 1)
        self.streams[eng].append((waits, fn, (eng, 1) if signal else None))
        self._commit(tok, reads, writes)

    def dma(self, q, out, in_, semkey, reads=(), writes=()):
        waits = self._deps(q, reads, writes)
        self._sem(semkey)
        self.cnt[semkey] += 16
        tok = (semkey, self.cnt[semkey])
        self.streams[q].append((waits, lambda e: e.dma_start(out=out, in_=in_), (semkey, 16)))
        self._commit(tok, reads, writes)

    def barrier(self):
        toks = [(k, c) for k, c in self.cnt.items() if c > 0]
        for e in self.engs:
            waits = []
            for k, v in toks:
                if k == e and not SAME_SYNC[e]:
                    continue
                if self.waited[e].get(k, 0) >= v:
                    continue
                self.waited[e][k] = v
                waits.append((k, v))
            if waits:
                self.streams[e].append((waits, None, None))

    def wait_all(self, eng, keys):
        waits = []
        for k in keys:
            if k in self.cnt and self.waited[eng].get(k, 0) < self.cnt[k]:
                self.waited[eng][k] = self.cnt[k]
                waits.append((k, self.cnt[k]))
        self.streams[eng].append((waits, None, None))

    def mm(self, out, lhsT, rhs, start, stop, reads, writes, signal):
        self.op('pe', lambda e: e.matmul(out, lhsT, rhs, start=start, stop=stop),
                reads, writes, signal)

    def transpose(self, out, in_, ident, reads, writes, signal):
        self.op('pe', lambda e: e.transpose(out, in_, ident), reads, writes, signal)

    def act(self, out, in_, func, reads, writes, **kw):
        self.op('act', lambda e: e.activation(out, in_, func, **kw), reads, writes)

    def copy(self, eng, out, in_, reads, writes):
        if eng == 'act':
            self.op('act', lambda e: e.copy(out, in_), reads, writes)
        else:
            self.op(eng, lambda e: e.tensor_copy(out, in_), reads, writes)

    def tt(self, out, in0, in1, op, reads, writes, eng='dve'):
        self.op(eng, lambda e: e.tensor_tensor(out, in0, in1, op), reads, writes)

    def ts(self, out, in0, s1, s2, op0, op1, reads, writes, eng='dve'):
        self.op(eng, lambda e: e.tensor_scalar(out, in0, s1, s2, op0, op1), reads, writes)

    def stt(self, out, in0, scalar, in1, op0, op1, reads, writes):
        self.op('dve', lambda e: e.scalar_tensor_tensor(out, in0, scalar, in1, op0, op1),
                reads, writes)

    def recip(self, out, in_, reads, writes):
        self.op('dve', lambda e: e.reciprocal(out, in_), reads, writes)

    def memset(self, eng, ap, val, writes):
        self.op(eng, lambda e: e.memset(ap, val), (), writes)

    def finalize(self):
        nc = self.nc
        with nc.Block() as block:
            def mk(name):
                def f(e):
                    for waits, fn, sig in self.streams[name]:
                        for k, v in waits:
                            e.wait_ge(self.sems[k], v)
                        if fn is None:
                            continue
                        inst = fn(e)
                        if sig is not None:
                            inst.then_inc(self.sems[sig[0]], sig[1])
                return f
            block.tensor(mk('pe'))
            block.scalar(mk('act'))
            block.vector(mk('dve'))
            block.gpsimd(mk('pool'))
            block.sync(mk('sp'))


class Arena:
    def __init__(self, nc, base, top, prog=None):
        self.nc, self.base, self.top, self.cur = nc, base, top, base
        self.peak = base
        self.prog = prog
        self.entries = []

    def alloc(self, name, shape, dt, fam=None):
        esz = 2 if dt == BF16 else 4
        n = esz
        for s_ in shape[1:]:
            n *= s_
        off = (self.cur + 31) // 32 * 32
        self.cur = off + n
        self.peak = max(self.peak, self.cur)
        assert self.cur <= self.top, "SBUF overflow at %s: %d > %d" % (name, self.cur, self.top)
        fam = fam or (name if name in ('kt1', 'kt2', 'sq2', 'x1') else name.rstrip('0123456789'))
        self.entries.append((off, off + n, fam))
        if self.prog is not None:
            self.prog.on_alloc(fam, off, off + n)
        return self.nc.alloc_sbuf_tensor_at(name, shape, dt, offset=off)

    def mark(self):
        return self.cur

    def skip(self, nbytes):
        self.cur += nbytes
        assert self.cur <= self.top

    def release(self, m):
        dead = [e for e in self.entries if e[0] >= m]
        self.entries = [e for e in self.entries if e[0] < m]
        if self.prog is not None:
            self.prog.on_release(dead)
        self.cur = m

    def kill(self, lo, hi):
        dead = [e for e in self.entries if lo <= e[0] and e[1] <= hi]
        self.entries = [e for e in self.entries if not (lo <= e[0] and e[1] <= hi)]
        if self.prog is not None:
            self.prog.on_release(dead)


def build(npass=2, stop_after=None):
    nc = bass.Bass("TRN2", target_bir_lowering=False)

    def din(name, shape):
        return nc.dram_tensor(name, shape, F32, kind="ExternalInput").ap()

    xseq = din("xseq", [SEQ, D])
    xext = din("xext", [2, E, D])
    emask = din("emask", [2, 128, 9])
    qtab = din("qtab", [2, 128, E])
    kcos = din("kcos", [128, SEQ])
    ksin = din("ksin", [128, SEQ])
    w_in = din("w_in", [D, 4160])
    w_qb = din("w_qb", [512, 1536])
    w_kvb = din("w_kvb", [512, 2048])
    w_o = din("w_o", [D, D])
    w_up = din("w_up", [D, 2 * DFF])
    w_dn = din("w_dn", [DFF, D])
    gbc3 = din("gbc3", [3, 128, D])
    ctab_d = din("ctab", [128, NCT])
    ident_d = din("ident", [128, 128])
    out = nc.dram_tensor("out", [2, NOWN, D], F32, kind="ExternalOutput").ap()

    P = Prog(nc)
    A = Arena(nc, (nc.sbuf_base + 63) // 64 * 64, nc.sbuf_top, P)
    psall = nc.alloc_psum_tensor("psall", [128, 8, 512], F32)
    ps = [psall[:, i, :] for i in range(8)]
    psbf = psall[:, :, :].bitcast(BF16)
    bank_rr = [0]

    def nextbank():
        b = bank_rr[0]
        bank_rr[0] = (b + 1) % 8
        return b

    ident = A.alloc("ident", [128, 128], F32)
    ones_bf = A.alloc("ones_bf", [128, 128], BF16, fam='ones')
    ctab = A.alloc("ctab", [128, NCT], F32)
    gbc = A.alloc("gbc", [128, D], F32)
    stat = A.alloc("stat", [128, 64], F32, fam='st')
    msk = A.alloc("msk", [128, 9], F32)
    P.dma('sp', ident[:, :], ident_d, 'ident', (), ['ident'])
    P.dma('sp', ctab[:, :], ctab_d, 'ctab', (), ['ctab'])
    P.memset('dve', ones_bf[:, :], 1.0, ['ones'])
    ident_bf = A.alloc("ident_bf", [128, 128], BF16, fam='identbf')
    P.copy('dve', ident_bf[:, :], ident[:, :], ['ident'], ['identbf'])
    ones_f = A.alloc("ones_f", [128, 128], F32, fam='onesf')
    P.memset('dve', ones_f[:, :], 1.0, ['onesf'])

    w_in_r = w_in.rearrange("(kc p) n -> p kc n", p=128)
    w_qb_r = w_qb.rearrange("(kc p) n -> p kc n", p=128)
    w_kvb_r = w_kvb.rearrange("(kc p) n -> p kc n", p=128)
    w_o_r = w_o.rearrange("(kc p) n -> p kc n", p=128)
    w_up_r = w_up.rearrange("(kc p) n -> p kc n", p=128)
    w_dn_r = w_dn.rearrange("(c p) n -> p c n", p=128)

    stat_rr = [0]

    def stat3():
        i = stat_rr[0]
        stat_rr[0] = (i + 1) % 16
        return i

    def load_gbc(which):
        P.dma('sp', gbc[:, :], gbc3[which], 'gbc', (), ['gbc'])

    def stage_a_head(src_ap, src_keys, rows, hs_bufs, hs_i, maskcol=None):
        si = stat3()
        ssq = stat[:rows, 4 * si:4 * si + 1]
        rs = stat[:rows, 4 * si + 1:4 * si + 2]
        rstd = stat[:rows, 4 * si + 2:4 * si + 3]
        sk = ('st', si)
        hs = hs_bufs[hs_i]
        hk = ('hs', hs_i)
        P.act(hs[:rows, :], src_ap, AF.Square, src_keys, [sk, hk], accum_out=ssq)
        P.act(rs, ssq, AF.Sqrt, [sk], [sk], scale=1.0 / D, bias=EPS)
        P.recip(rstd, rs, [sk], [sk])
        if maskcol is not None:
            P.tt(rstd, rstd, msk[:rows, maskcol:maskcol + 1], ALU.mult, [sk, 'msk'], [sk])
        P.stt(hs[:rows, :], src_ap, rstd, gbc[:rows, :], ALU.mult, ALU.mult,
              list(src_keys) + [sk, 'gbc'], [hk])
        return hs, hk

    def stage_a_tail(hs, hk, rows, dst_fn, dst_key):
        for q in range(4):
            b = nextbank()
            for j in range(4):
                kc = 4 * q + j
                P.transpose(psbf[:, b, j * 128:j * 128 + rows], hs[:rows, kc * 128:(kc + 1) * 128],
                            ident_bf[:rows, :rows], [hk, 'identbf'], [('ps', b)], signal=(j == 3))
            src = psbf[:, b, 0:512].rearrange("p (j c) -> p j c", j=4)[:, :, 0:rows]
            P.copy('dve' if q == 3 else 'act', dst_fn(4 * q), src, [('ps', b)], [dst_key])

    def run_tiles(specs, hs_bufs, after_tail=None):
        nb = len(hs_bufs)
        heads = {}

        def do_head(i):
            sp = specs[i]
            if sp.get('load') is not None:
                sp['load']()
            heads[i] = stage_a_head(sp['src'], sp['keys'], sp['rows'], hs_bufs, i % nb, sp.get('maskcol'))
        ahead = nb - 1
        for i in range(min(ahead, len(specs))):
            do_head(i)
        for i, sp in enumerate(specs):
            if i + ahead < len(specs):
                do_head(i + ahead)
            hs, hk = heads.pop(i)
            stage_a_tail(hs, hk, sp['rows'], sp['dst_fn'], sp['dst_key'])
            if after_tail is not None:
                after_tail(i)

    pass_base = A.mark()
    for ps_i in range(npass):
        A.release(pass_base)
        R1SIZE = 41472
        pb64 = (pass_base + 63) // 64 * 64
        AR1 = Arena(nc, pb64, pb64 + R1SIZE, P)
        A.cur = pb64
        A.skip(R1SIZE)
        r1_end = A.mark()
        ckvnT = A.alloc("ckvnT", [128, 4, SEQ], BF16)
        kropeT = A.alloc("kropeT", [128, SEQ], BF16)
        cqnT = A.alloc("cqnT", [128, 4, E], BF16)
        r2_end = A.mark()

        P.dma('sp', msk[:, :], emask[ps_i], 'msk', (), ['msk'])

        wkv = AR1.alloc("wkv", [128, KC, 768], BF16)
        xt = [A.alloc("xt%d" % i, [128, D], F32) for i in range(3)]
        hsb = [A.alloc("hs%d" % i, [128, D], BF16) for i in range(3)]
        junk = None
        hTkv = [A.alloc("hTkv%d" % i, [128, KC, 512], BF16) for i in range(2)]
        ckvf = A.alloc("ckvf", [128, 4, 512], F32)
        sqb = A.alloc("sqb", [128, 4, 512], BF16)
        rsb = A.alloc("rsb", [128, 512], F32)
        rstdb = A.alloc("rstdb", [128, 512], F32)
        kcs = [A.alloc("kcs%d" % i, [128, 512], F32) for i in range(1)]
        ksn = [A.alloc("ksn%d" % i, [128, 512], F32) for i in range(1)]
        kt1 = A.alloc("kt1", [128, 512], F32)
        kt2 = A.alloc("kt2", [128, 512], F32)

        load_gbc(0)
        P.dma('pool', wkv[:, :, 0:576], w_in_r[:, :, 512:1088], 'wkv', (), ['wkv'])
        P.dma('pool', wkv[:, :, 576:640], w_in_r[:, :, 1024:1088], 'wkv', (), ['wkv'])
        P.copy('dve', wkv[:, :, 640:672], wkv[:, :, 544:576], ['wkv'], ['wkv'])
        P.copy('dve', wkv[:, :, 672:704], wkv[:, :, 512:544], ['wkv'], ['wkv'])
        P.copy('dve', wkv[:, :, 704:768], wkv[:, :, 640:704], ['wkv'], ['wkv'])

        def kv_units(kb):
            hb = kb % 2
            st = {}
            us = []
            for m in range(4):
                def pe(m=m):
                    b = nextbank()
                    st[m] = b
                    for k in range(KC):
                        P.mm(ps[b][:, :], wkv[:, k, m * 128:(m + 1) * 128], hTkv[hb][:, k, :],
                             k == 0, k == KC - 1, ['wkv', ('hTkv', hb)], [('ps', b)], k == KC - 1)

                def ev(m=m):
                    b = st[m]
                    P.copy('act', ckvf[:, m, :], ps[b][:, :], [('ps', b)], [('ckvf', m)])
                    P.act(sqb[:, m, :], ps[b][:, :], AF.Square, [('ps', b)], [('sqb', m)])
                us.append((pe, ev))

            def pe_n():
                b = nextbank()
                st['n'] = b
                for m in range(4):
                    P.mm(ps[b][:, :], ones_bf[:, :], sqb[:, m, :], m == 0, m == 3,
                         ['ones', ('sqb', m)], [('ps', b)], m == 3)

            def ev_n():
                b = st['n']
                P.act(rsb[:, :], ps[b][:, :], AF.Sqrt, [('ps', b)], ['rsb'], scale=1.0 / 512, bias=EPS)
                P.recip(rstdb[:, :], rsb[:, :], ['rsb'], ['rstdb'])
                for m in range(4):
                    P.stt(ckvnT[:, m, kb * 512:(kb + 1) * 512], ckvf[:, m, :],
                          ctab[:, C_GKV + m:C_GKV + m + 1], rstdb[:, :], ALU.mult, ALU.mult,
                          [('ckvf', m), 'ctab', 'rstdb'], [('ckvnT', kb)])

            def pe_r1():
                P.dma('sp', kcs[0][:, :], kcos[:, kb * 512:(kb + 1) * 512], ('kcs', 0), (), [('kcs', 0)])
                P.dma('sp', ksn[0][:, :], ksin[:, kb * 512:(kb + 1) * 512], ('ksn', 0), (), [('ksn', 0)])
                b1 = nextbank()
                st['r1'] = b1
                for k in range(KC):
                    P.mm(ps[b1][:, :], wkv[:, k, 512:640], hTkv[hb][:, k, :], k == 0, k == KC - 1,
                         ['wkv', ('hTkv', hb)], [('ps', b1)], k == KC - 1)

            def ev_r1():
                b1 = st['r1']
                P.tt(kt1[:, :], ps[b1][:, :], kcs[0][:, :], ALU.mult, [('ps', b1), ('kcs', 0)], ['kt1'])

            def pe_r2():
                b2 = nextbank()
                st['r2'] = b2
                for k in range(KC):
                    P.mm(ps[b2][:, :], wkv[:, k, 640:768], hTkv[hb][:, k, :], k == 0, k == KC - 1,
                         ['wkv', ('hTkv', hb)], [('ps', b2)], k == KC - 1)

            def ev_r2():
                b2 = st['r2']
                P.tt(kt2[:, :], ps[b2][:, :], ksn[0][:, :], ALU.mult, [('ps', b2), ('ksn', 0)], ['kt2'])
                P.tt(kropeT[:, kb * 512:(kb + 1) * 512], kt1[:, :], kt2[:, :], ALU.add,
                     ['kt1', 'kt2'], [('kropeT', kb)])
            us2 = us + [(pe_r1, ev_r1), (pe_r2, ev_r2), (pe_n, ev_n)]
            return us2

        uq = []
        evq = []

        def pump(nunits):
            while evq:
                evq.pop(0)()
            for _ in range(nunits):
                if uq:
                    pe, ev = uq.pop(0)
                    pe()
                    evq.append(ev)

        def after_tail(i):
            if i % 4 == 3:
                uq.extend(kv_units(i // 4))
            pump(2)

        specs = []
        for i in range(32):
            kb, t = i // 4, i % 4
            xi = i % 3

            def ld(i=i, xi=xi):
                P.dma('sp', xt[xi][:, :], xseq[i * 128:(i + 1) * 128, :], ('xt', xi), (), [('xt', xi)])
            specs.append(dict(load=ld, src=xt[xi][:, :], keys=[('xt', xi)], rows=128,
                              dst_fn=(lambda kc0, hb=kb % 2, t=t: hTkv[hb][:, kc0:kc0 + 4, t * 128:(t + 1) * 128]),
                              dst_key=('hTkv', kb % 2)))
        run_tiles(specs, hsb, after_tail=after_tail)
        while uq or evq:
            pump(1)
        A.release(r2_end)
        AR1.release(AR1.base)
        ycT = AR1.alloc("ycT", [128, 8, E], BF16)
        aT = AR1.alloc("aT", [128, 8, E], BF16)
        sqaccA = AR1.alloc("sqaccA", [128, E], F32)
        sqaccB = AR1.alloc("sqaccB", [128, E], F32)
        if stop_after == 'P0':
            break

        hT = A.alloc("hT", [128, KC, E], BF16)
        p1_mark = A.mark()
        xt = [A.alloc("xt%d" % i, [128, D], F32) for i in range(3)]
        hsb = [A.alloc("hs%d" % i, [128, D], BF16) for i in range(3)]
        junk = None
        specs = []
        for t, (c0, rows) in enumerate(TILES):
            xi = t % 3

            def ld(t=t, xi=xi, c0=c0, rows=rows):
                P.dma('sp', xt[xi][:rows, :], xext[ps_i, c0:c0 + rows, :], ('xt', xi), (), [('xt', xi)])
            specs.append(dict(load=ld, src=xt[xi][:rows, :], keys=[('xt', xi)], rows=rows,
                              dst_fn=(lambda kc0, c0=c0, rows=rows: hT[:, kc0:kc0 + 4, c0:c0 + rows]),
                              dst_key=('hT', t)))
        run_tiles(specs, hsb)
        A.release(p1_mark)
        cqf = A.alloc("cqf", [128, 4, E], F32)
        sqq = A.alloc("sqq", [128, 4, E], BF16)
        win = [A.alloc("win%d" % i, [128, KC, 128], BF16) for i in range(3)]
        Cf = A.alloc("Cf", [128, E], F32)
        CHc = A.alloc("CHc", [128, E], F32)
        Bc = A.alloc("Bc", [128, E], F32)
        yf = A.alloc("yf", [128, 1026], F32)
        sqt = A.alloc("sqt", [128, 1026], F32)
        rsq = A.alloc("rsq", [128, 343], F32)
        rstdq = A.alloc("rstdq", [128, 343], F32)

        wcnt = [0]

        def in_chunk(col0, evac):
            wi = wcnt[0] % 3
            wcnt[0] += 1
            P.dma('pool', win[wi][:, :, :], w_in_r[:, :, col0:col0 + 128], ('win', wi), (), [('win', wi)])
            for g, (c0, n) in enumerate(GROUPS):
                b = nextbank()
                rk = [('win', wi)] + [('hT', t) for t in tiles_of(c0, n)]
                for k in range(KC):
                    P.mm(ps[b][:, 0:n], win[wi][:, k, :], hT[:, k, c0:c0 + n], k == 0, k == KC - 1,
                         rk, [('ps', b)], k == KC - 1)
                evac(g, c0, n, b)

        for m in range(4):
            def ev(g, c0, n, b, m=m):
                P.copy('act', cqf[:, m, c0:c0 + n], ps[b][:, 0:n], [('ps', b)], [('cqf', m, g)])
                P.act(sqq[:, m, c0:c0 + n], ps[b][:, 0:n], AF.Square, [('ps', b)], [('sqq', m, g)])
            in_chunk(m * 128, ev)
        for g, (c0, n) in enumerate(GROUPS):
            b = nextbank()
            for m in range(4):
                P.mm(ps[b][:, 0:n], ones_bf[:, :], sqq[:, m, c0:c0 + n], m == 0, m == 3,
                     ['ones', ('sqq', m, g)], [('ps', b)], m == 3)
            P.act(rsq[:, 0:n], ps[b][:, 0:n], AF.Sqrt, [('ps', b)], ['rsq'], scale=1.0 / 512, bias=EPS)
            P.recip(rstdq[:, 0:n], rsq[:, 0:n], ['rsq'], ['rstdq'])
            for m in range(4):
                P.stt(cqnT[:, m, c0:c0 + n], cqf[:, m, c0:c0 + n], ctab[:, C_GQ + m:C_GQ + m + 1],
                      rstdq[:, 0:n], ALU.mult, ALU.mult, [('cqf', m, g), 'ctab', 'rstdq'], [('cqnT', g)])

        def ext_to_conv(dst, src_fn, g, c0, n, b, eng, wkey, extra_reads=()):
            hi = min(c0 + n, 1024)
            src_fn(dst[:, 2 + c0:2 + hi], c0, hi, 0)
            if c0 + n > 1024:
                src_fn(dst[:, 0:2], 1024, 1026, 1)
                src_fn(dst[:, 1026:1028], 1026, 1028, 2)

        P.memset('dve', ycT[:, :, 1024:1025], 0.0, [('ycT', 'pad')])
        P.memset('dve', ycT[:, :, 1027:1028], 0.0, [('ycT', 'pad')])
        for c in range(8):
            def evC(g, c0, n, b):
                P.copy('act', Cf[:, c0:c0 + n], ps[b][:, 0:n], [('ps', b)], [('Cf', g)])
            in_chunk(1088 + 1024 + c * 128, evC)

            def evH(g, c0, n, b):
                def f(dst, lo, hi, part):
                    P.tt(dst, ps[b][:, lo - c0:hi - c0], Cf[:, lo:hi], ALU.mult,
                         [('ps', b), ('Cf', g)], [('CHc', g, part)])
                ext_to_conv(CHc, f, g, c0, n, b, 'dve', 'CHc')
            in_chunk(1088 + 2048 + c * 128, evH)

            def evB(g, c0, n, b):
                def f(dst, lo, hi, part):
                    P.copy('act', dst, ps[b][:, lo - c0:hi - c0], [('ps', b)], [('Bc', g, part)])
                ext_to_conv(Bc, f, g, c0, n, b, 'act', 'Bc')
            in_chunk(1088 + c * 128, evB)

            chk = [('CHc', g, p) for g in range(3) for p in range(3 if g == 2 else 1)]
            bck = [('Bc', g, p) for g in range(3) for p in range(3 if g == 2 else 1)]
            w0 = ctab[:, C_SCW + c:C_SCW + c + 1]
            w1 = ctab[:, C_SCW + 8 + c:C_SCW + 8 + c + 1]
            w2 = ctab[:, C_SCW + 16 + c:C_SCW + 16 + c + 1]
            P.ts(yf[:, :], CHc[:, 1:1027], w1, None, ALU.mult, ALU.bypass, chk + ['ctab'], ['yf'])
            P.stt(yf[:, :], CHc[:, 0:1026], w0, yf[:, :], ALU.mult, ALU.add, chk + ['ctab', 'yf'], ['yf'])
            P.stt(yf[:, :], CHc[:, 2:1028], w2, yf[:, :], ALU.mult, ALU.add, chk + ['ctab', 'yf'], ['yf'])
            P.tt(yf[:, :], yf[:, :], Bc[:, 1:1027], ALU.mult, ['yf'] + bck, ['yf'])
            gB = ctab[:, C_GB + c:C_GB + c + 1]
            P.ts(ycT[:, c, 0:1024], yf[:, 1:1025], gB, None, ALU.mult, ALU.bypass,
                 ['yf', 'ctab'], [('ycT', c)])
            P.ts(ycT[:, c, 1025:1027], yf[:, 0:1026:1025], gB, None, ALU.mult, ALU.bypass,
                 ['yf', 'ctab'], [('ycT', c)])
            if c == 0:
                P.memset('dve', sqaccB[:, 1024:1028], 0.0, ['sqaccB'])
                P.tt(sqaccB[:, 0:1024], yf[:, 1:1025], yf[:, 1:1025], ALU.mult, ['yf'], ['sqaccB'])
                P.tt(sqaccB[:, 1025:1027], yf[:, 0:1026:1025], yf[:, 0:1026:1025], ALU.mult,
                     ['yf'], ['sqaccB'])
            else:
                P.tt(sqt[:, :], yf[:, :], yf[:, :], ALU.mult, ['yf'], ['sqt'])
                P.tt(sqaccB[:, 0:1024], sqaccB[:, 0:1024], sqt[:, 1:1025], ALU.add,
                     ['sqt', 'sqaccB'], ['sqaccB'])
                P.tt(sqaccB[:, 1025:1027], sqaccB[:, 1025:1027], sqt[:, 0:1026:1025], ALU.add,
                     ['sqt', 'sqaccB'], ['sqaccB'])
        A.release(r2_end)
        if stop_after == 'P1':
            break

        KhT = [A.alloc("KhT%d" % i, [128, SEQ], BF16) for i in range(2)]
        Vh = [A.alloc("Vh%d" % i, [128, 32, 128], BF16) for i in range(2)]
        A.alloc("padP2", [128, 64], BF16)
        wkvb = A.alloc("wkvb", [128, 4, 2048], BF16)
        wqb = A.alloc("wqb", [128, 4, 1536], BF16)
        wqr = A.alloc("wqr", [128, 4, NH, 128], BF16)
        QhN = [A.alloc("QhN%d" % i, [128, E], BF16) for i in range(2)]
        QhR = [A.alloc("QhR%d" % i, [128, E], BF16) for i in range(2)]
        NPT = 5
        PT = [A.alloc("PT%d" % i, [128, 2, 344], BF16) for i in range(NPT)]
        accD = A.alloc("accD", [128, 2, 344], F32)
        accP = A.alloc("accP", [128, 2, 344], F32)
        tsum = [A.alloc("tsum%d" % i, [128, 344], F32) for i in range(2)]
        rec = A.alloc("rec", [128, 344], F32)
        attn = A.alloc("attn", [128, 344], F32)
        sq2 = A.alloc("sq2", [128, 344], F32)
        qT = A.alloc("qT", [128, E], F32)

        P.dma('pool', wkvb[:, :, :], w_kvb_r, 'wkvb', (), ['wkvb'])
        P.dma('pool', wqb[:, :, :], w_qb_r, 'wqb', (), ['wqb'])
        wqbv = wqb[:, :, :].rearrange("p k (h d) -> p k h d", d=192)
        P.copy('dve', wqr[:, :, :, 0:64], wqbv[:, :, :, 128:192], ['wqb'], ['wqr'])
        P.copy('dve', wqr[:, :, :, 64:96], wqbv[:, :, :, 160:192], ['wqb'], ['wqr'])
        P.copy('dve', wqr[:, :, :, 96:128], wqbv[:, :, :, 128:160], ['wqb'], ['wqr'])
        P.dma('sp', qT[:, :], qtab[ps_i], 'qT', (), ['qT'])

        SPAIR, OB, MB = [0, 2], [4, 5], [6, 7]
        mrr = [0]

        def miscbank():
            b = MB[mrr[0] % 2]
            mrr[0] += 1
            return b

        def expansion_units(h):
            hb = h % 2
            us = []
            for kb in range(8):
                def uK(kb=kb):
                    b = miscbank()
                    for k in range(4):
                        P.mm(ps[b][:, :], wkvb[:, k, h * 256:h * 256 + 128],
                             ckvnT[:, k, kb * 512:(kb + 1) * 512], k == 0, k == 3,
                             ['wkvb', ('ckvnT', kb)], [('ps', b)], k == 3)
                    P.copy('dve', KhT[hb][:, kb * 512:(kb + 1) * 512], ps[b][:, :], [('ps', b)], [('KhT', hb, kb)])
                us.append(uK)

                def uV(kb=kb):
                    b = miscbank()
                    for t in range(4):
                        tok = kb * 512 + t * 128
                        for k in range(4):
                            P.mm(ps[b][:, t * 128:(t + 1) * 128], ckvnT[:, k, tok:tok + 128],
                                 wkvb[:, k, h * 256 + 128:h * 256 + 256], k == 0, k == 3,
                                 ['wkvb', ('ckvnT', kb)], [('ps', b)], (k == 3 and t == 3))
                    P.copy('dve', Vh[hb][:, kb * 4:(kb + 1) * 4, :],
                           ps[b][:, :].rearrange("p (t c) -> p t c", t=4), [('ps', b)], [('Vh', hb, kb)])
                us.append(uV)
            for g, (c0, n) in enumerate(GROUPS):
                def uQ1(g=g, c0=c0, n=n):
                    b = miscbank()
                    for k in range(4):
                        P.mm(ps[b][:, 0:n], wqbv[:, k, h, 0:128], cqnT[:, k, c0:c0 + n],
                             k == 0, k == 3, ['wqb', ('cqnT', g)], [('ps', b)], k == 3)
                    P.copy('dve', QhN[hb][:, c0:c0 + n], ps[b][:, 0:n], [('ps', b)], [('QhN', hb, g)])
                us.append(uQ1)

                def uQ2(g=g, c0=c0, n=n):
                    b1 = miscbank()
                    for k in range(4):
                        P.mm(ps[b1][:, 0:n], wqr[:, k, h, :], cqnT[:, k, c0:c0 + n],
                             k == 0, k == 3, ['wqr', ('cqnT', g)], [('ps', b1)], k == 3)
                    P.tt(QhR[hb][:, c0:c0 + n], ps[b1][:, 0:n], qT[:, c0:c0 + n], ALU.mult,
                         [('ps', b1), 'qT'], [('QhR', hb, g)])
                us.append(uQ2)
            return us

        for u in expansion_units(0):
            u()
        gcount = 0
        epi_pending = []
        for h in range(NH):
            hb = h % 2
            pend = expansion_units(h + 1) if h + 1 < NH else []
            npend = len(pend)
            nsteps = 3 * 16
            step = 0
            for g, (c0, n) in enumerate(GROUPS):
                ob = OB[gcount % 2]
                gcount += 1

                def qkpair(j):
                    a = SPAIR[j % 2]
                    for u in range(2):
                        kc = 2 * j + u
                        kb = kc // 4
                        P.mm(ps[a + u][:, 0:n], KhT[hb][:, kc * 128:(kc + 1) * 128], QhN[hb][:, c0:c0 + n],
                             True, False, [('KhT', hb, kb), ('QhN', hb, g)], [('ps', a + u)], False)
                        P.mm(ps[a + u][:, 0:n], kropeT[:, kc * 128:(kc + 1) * 128], QhR[hb][:, c0:c0 + n],
                             False, True, [('kropeT', kb), ('QhR', hb, g)],
                             [('ps', a + u)], True)
                    pi = (gcount * 16 + j) % NPT
                    P.act(PT[pi][:, :, 0:n], psall[:, a:a + 2, 0:n], AF.Exp,
                          [('ps', a), ('ps', a + 1)], [('PT', pi)], scale=SCALE)

                def pvpair(j):
                    pi = (gcount * 16 + j) % NPT
                    for u in range(2):
                        kc = 2 * j + u
                        P.mm(ps[ob][:, 0:n], Vh[hb][:, kc, :], PT[pi][:, u, 0:n], kc == 0, kc == 31,
                             [('Vh', hb, kc // 4), ('PT', pi)], [('ps', ob)], u == 1)
                    if j % 2 == 1:
                        eng, acc, nm = 'pool', accP, 'accP'
                        first = (j == 1)
                    else:
                        eng, acc, nm = 'dve', accD, 'accD'
                        first = (j == 0)
                    if first:
                        P.copy(eng, acc[:, :, 0:n], PT[pi][:, :, 0:n], [('PT', pi)], [nm])
                    else:
                        P.tt(acc[:, :, 0:n], acc[:, :, 0:n], PT[pi][:, :, 0:n], ALU.add,
                             [nm, ('PT', pi)], [nm], eng=eng)

                qkpair(0)
                for j in range(16):
                    if j + 1 < 16:
                        qkpair(j + 1)
                    pvpair(j)
                    step += 1
                    if j == 3 and epi_pending:
                        epi_pending.pop(0)()
                    while pend and (npend - len(pend)) < step * npend // nsteps:
                        pend.pop(0)()
                tsb = tsum[gcount % 2]
                tsk = ('tsum', gcount % 2)
                P.tt(tsb[:, 0:n], accD[:, 0, 0:n], accD[:, 1, 0:n], ALU.add, ['accD'], [tsk])
                P.tt(tsb[:, 0:n], tsb[:, 0:n], accP[:, 0, 0:n], ALU.add, [tsk, 'accP'], [tsk])
                P.tt(tsb[:, 0:n], tsb[:, 0:n], accP[:, 1, 0:n], ALU.add, [tsk, 'accP'], [tsk])

                def epi_b(h=h, g=g, c0=c0, n=n, ob=ob, tsb=tsb, tsk=tsk):
                    db = miscbank()
                    P.mm(ps[db][:, 0:n], ones_f[:, :], tsb[:, 0:n], True, True, ['onesf', tsk], [('ps', db)], True)
                    P.recip(rec[:, 0:n], ps[db][:, 0:n], [('ps', db)], ['rec'])
                    P.tt(attn[:, 0:n], ps[ob][:, 0:n], rec[:, 0:n], ALU.mult, [('ps', ob), 'rec'], ['attn'])
                    P.ts(aT[:, h, c0:c0 + n], attn[:, 0:n], ctab[:, C_GA + h:C_GA + h + 1], None,
                         ALU.mult, ALU.bypass, ['attn', 'ctab'], [('aT', h, g)])
                    if h == 0:
                        P.tt(sqaccA[:, c0:c0 + n], attn[:, 0:n], attn[:, 0:n], ALU.mult, ['attn'], [('sqaccA', g)])
                    else:
                        P.tt(sq2[:, 0:n], attn[:, 0:n], attn[:, 0:n], ALU.mult, ['attn'], ['sq2'])
                        P.tt(sqaccA[:, c0:c0 + n], sqaccA[:, c0:c0 + n], sq2[:, 0:n], ALU.add,
                             ['sq2', ('sqaccA', g)], [('sqaccA', g)])
                epi_pending.append(epi_b)
            while pend:
                pend.pop(0)()
        while epi_pending:
            epi_pending.pop(0)()
        A.release(r1_end)
        if stop_after == 'P2':
            break

        x1 = A.alloc("x1", [128, 8, D], F32)
        h2T = A.alloc("h2T", [128, KC, E], BF16)
        x1h = A.alloc("x1h", [128, D], F32, fam='x1')
        p3_mark = A.mark()
        wo = [A.alloc("wo%d" % i, [128, KC, 512], BF16) for i in range(2)]
        sqbA = A.alloc("sqbA", [128, E], BF16)
        sqbB = A.alloc("sqbB", [128, E], BF16)
        rstdAB = A.alloc("rstdAB", [128, 9, 4], F32, fam='rAB')

        def x1tile(t):
            return x1[:, t, :] if t < 8 else x1h[0:4, :]

        def x1_load(t, after=None):
            c0, rows = TILES[t]
            dst = x1[:rows, t, :] if t < 8 else x1h[0:4, :]
            P.dma('sp', dst, xext[ps_i, c0:c0 + rows, :], ('x1ld', t),
                  [('x1', after, 0)] if after is not None else (), [('x1', t, fb) for fb in range(4)])
        x1_load(0)
        x1_load(1)
        P.copy('dve', sqbA[:, :], sqaccA[:, :], [('sqaccA', g) for g in range(3)], ['sqbA'])
        P.copy('dve', sqbB[:, :], sqaccB[:, :], ['sqaccB'], ['sqbB'])
        for t, (c0, rows) in enumerate(TILES):
            for j, (sqx, nm) in enumerate(((sqbA, 'sqbA'), (sqbB, 'sqbB'))):
                b = nextbank()
                P.mm(ps[b][:rows, 0:1], sqx[:, c0:c0 + rows], ones_bf[:, 0:1], True, True,
                     [nm, 'ones'], [('ps', b)], True)
                P.act(rstdAB[:rows, t, 2 + j:3 + j], ps[b][:rows, 0:1], AF.Sqrt, [('ps', b)],
                      [('rAB', t, j)], scale=1.0 / 1024, bias=EPS)
                P.recip(rstdAB[:rows, t, j:j + 1], rstdAB[:rows, t, 2 + j:3 + j], [('rAB', t, j)], [('rAB', t, j)])
        for fb in range(4):
            wi = fb % 2
            P.dma('pool', wo[wi][:, :, :], w_o_r[:, :, fb * 512:(fb + 1) * 512], ('wo', wi), (), [('wo', wi)])
            for t, (c0, rows) in enumerate(TILES):
                gs = [g for g, (g0, gn) in enumerate(GROUPS) if g0 < c0 + rows and g0 + gn > c0]
                ba = nextbank()
                for hh in range(8):
                    P.mm(ps[ba][:rows, :], aT[:, hh, c0:c0 + rows], wo[wi][:, hh, :], hh == 0, hh == 7,
                         [('aT', hh, g) for g in gs] + [('wo', wi)], [('ps', ba)], hh == 7)
                bb = nextbank()
                for c in range(8):
                    P.mm(ps[bb][:rows, :], ycT[:, c, c0:c0 + rows], wo[wi][:, 8 + c, :], c == 0, c == 7,
                         [('ycT', c), ('ycT', 'pad'), ('wo', wi)], [('ps', bb)], c == 7)
                xs = (x1[:rows, t, fb * 512:(fb + 1) * 512] if t < 8 else x1h[0:4, fb * 512:(fb + 1) * 512])
                P.stt(xs, ps[ba][:rows, :], rstdAB[:rows, t, 0:1], xs, ALU.mult, ALU.add,
                      [('ps', ba), ('rAB', t, 0), ('x1', t, fb)], [('x1', t, fb)])
                P.stt(xs, ps[bb][:rows, :], rstdAB[:rows, t, 1:2], xs, ALU.mult, ALU.add,
                      [('ps', bb), ('rAB', t, 1), ('x1', t, fb)], [('x1', t, fb)])
                if fb == 0 and t + 2 < len(TILES):
                    x1_load(t + 2, after=t)
        A.release(p3_mark)
        hsb = [A.alloc("hs%d" % i, [128, D], BF16) for i in range(3)]
        junk = None
        load_gbc(1)
        specs = []
        for t, (c0, rows) in enumerate(TILES):
            src = x1[:rows, t, :] if t < 8 else x1h[0:4, :]
            specs.append(dict(load=None, src=src, keys=[('x1', t, fb) for fb in range(4)], rows=rows,
                              dst_fn=(lambda kc0, c0=c0, rows=rows: h2T[:, kc0:kc0 + 4, c0:c0 + rows]),
                              dst_key=('h2T', t), maskcol=t))
        run_tiles(specs, hsb)
        A.release(p3_mark)
        if stop_after == 'P3b':
            break

        AR1.release(AR1.base)
        wug = [AR1.alloc("wug%d" % i, [128, KC, 256], BF16) for i in range(2)]
        wuv = [AR1.alloc("wuv%d" % i, [128, KC, 256], BF16) for i in range(2)]
        ug = [A.alloc("ug%d" % i, [128, 1026], F32) for i in range(2)]
        uv = [A.alloc("uv%d" % i, [128, 1026], F32) for i in range(2)]
        wdn = [A.alloc("wdn%d" % i, [128, 2, D], BF16) for i in range(2)]
        tg = A.alloc("tg", [128, 1024], F32)
        tv = A.alloc("tv", [128, 1024], F32)
        actT = [A.alloc("actT%d" % i, [128, 2, 1024], BF16) for i in range(2)]

        h2keys = {g: [('h2T', t) for t in tiles_of(c0, n)] for g, (c0, n) in enumerate(GROUPS)}

        def up_dma(s):
            si = s % 2
            P.dma('pool', wug[si][:, :, :], w_up_r[:, :, 256 * s:256 * s + 256], ('wug', si), (), [('wug', si)])
            P.dma('pool', wuv[si][:, :, :], w_up_r[:, :, DFF + 256 * s:DFF + 256 * s + 256], ('wuv', si), (), [('wuv', si)])

        def dn_dma(s):
            si = s % 2
            P.dma('pool', wdn[si][:, :, :], w_dn_r[:, 2 * s:2 * s + 2, :], ('wdn', si), (), [('wdn', si)])

        def up(s, inter=()):
            inter = list(inter)
            left = [12]

            def spread():
                left[0] -= 1
                if left[0] >= 8:
                    return
                k = -(-len(inter) // (left[0] + 1))
                for _ in range(min(k, len(inter))):
                    inter.pop(0)()
            si = s % 2
            for jj in range(2):
                ui = (2 * s + jj) % 2
                for (wbuf, wnm, ubuf, unm) in ((wug, 'wug', ug, 'ug'), (wuv, 'wuv', uv, 'uv')):
                    for g, (c0, n) in enumerate(GROUPS):
                        b = nextbank()
                        for k in range(KC):
                            P.mm(ps[b][:, 0:n], wbuf[si][:, k, jj * 128:(jj + 1) * 128], h2T[:, k, c0:c0 + n],
                                 k == 0, k == KC - 1, [(wnm, si)] + h2keys[g], [('ps', b)], k == KC - 1)
                        hi = min(c0 + n, 1024)
                        P.copy('act', ubuf[ui][:, 1 + c0:1 + hi], ps[b][:, 0:hi - c0], [('ps', b)], [(unm, ui, g, 0)])
                        if c0 + n > 1024:
                            P.copy('act', ubuf[ui][:, 0:1026:1025], ps[b][:, 1025 - c0:1027 - c0],
                                   [('ps', b)], [(unm, ui, g, 1)])
                        spread()
                J = 2 * s + jj
                for (ubuf, unm, tbuf, tnm, Jg) in ((ug, 'ug', tg, 'tg', J), (uv, 'uv', tv, 'tv', 44 + J)):
                    uk = [(unm, ui, g, p) for g in range(3) for p in range(2 if g == 2 else 1)]
                    w0 = ctab[:, C_FCW + Jg:C_FCW + Jg + 1]
                    w1 = ctab[:, C_FCW + 88 + Jg:C_FCW + 88 + Jg + 1]
                    w2 = ctab[:, C_FCW + 176 + Jg:C_FCW + 176 + Jg + 1]
                    bb_ = ctab[:, C_FCB + Jg:C_FCB + Jg + 1]
                    P.act(tbuf[:, :], ubuf[ui][:, 1:1025], AF.Identity, uk + ['ctab'], [tnm], scale=w1, bias=bb_)
                    P.stt(tbuf[:, :], ubuf[ui][:, 0:1024], w0, tbuf[:, :], ALU.mult, ALU.add,
                          uk + ['ctab', tnm], [tnm])
                    P.stt(tbuf[:, :], ubuf[ui][:, 2:1026], w2, tbuf[:, :], ALU.mult, ALU.add,
                          uk + ['ctab', tnm], [tnm])
                P.act(tg[:, :], tg[:, :], AF.Silu, ['tg'], ['tg'])
                P.tt(actT[si][:, jj, :], tg[:, :], tv[:, :], ALU.mult, ['tg', 'tv'], [('actT', si, jj)], eng='pool')
            while inter:
                inter.pop(0)()

        def down_groups(s):
            si = s % 2
            gs = []
            for t in range(8):
                for fb in range(4):
                    def grp(t=t, fb=fb):
                        b = nextbank()
                        for jj in range(2):
                            P.mm(ps[b][:, :], actT[si][:, jj, t * 128:(t + 1) * 128],
                                 wdn[si][:, jj, fb * 512:(fb + 1) * 512], jj == 0, jj == 1,
                                 [('actT', si, jj), ('wdn', si)], [('ps', b)], jj == 1)
                        xs = x1[:, t, fb * 512:(fb + 1) * 512]
                        P.tt(xs, xs, ps[b][:, :], ALU.add, [('ps', b), ('x1', t, fb)], [('x1', t, fb)])
                    gs.append(grp)
            return gs

        up_dma(0)
        up_dma(1)
        dn_dma(0)
        up(0)
        for s in range(NSUP):
            if s + 2 < NSUP:
                up_dma(s + 2)
            if s + 1 < NSUP:
                dn_dma(s + 1)
                up(s + 1, inter=down_groups(s))
            else:
                for g_ in down_groups(s):
                    g_()
        A.release(p3_mark)
        AR1.release(AR1.base)

        ob_ = [A.alloc("ob%d" % i, [128, D], F32) for i in range(2)]
        load_gbc(2)
        for t in reversed(range(8)):
            si = stat3()
            ssq = stat[:, 4 * si:4 * si + 1]
            rs = stat[:, 4 * si + 1:4 * si + 2]
            rstd = stat[:, 4 * si + 2:4 * si + 3]
            sk = ('st', si)
            xk = [('x1', t, fb) for fb in range(4)]
            oi = t % 2
            P.act(ob_[oi][:, :], x1[:, t, :], AF.Square, xk, [sk, ('ob', oi)], accum_out=ssq)
            P.act(rs, ssq, AF.Sqrt, [sk], [sk], scale=1.0 / D, bias=EPS)
            P.recip(rstd, rs, [sk], [sk])
            P.stt(ob_[oi][:, :], x1[:, t, :], rstd, gbc[:, :], ALU.mult, ALU.mult,
                  xk + [sk, 'gbc'], [('ob', oi)])
            P.dma('sp', out[ps_i, t * 128:(t + 1) * 128, :], ob_[oi][:, :], ('ost', oi), [('ob', oi)], [('ob', oi, 'st')])
        A.release(pass_base)

    P.wait_all('sp', [k for k in P.cnt if isinstance(k, tuple) and k[0] == 'ost'])
    P.finalize()
    return nc, A.peak


_CACHE = {}


def _rope_tables():
    pos = np.arange(SEQ, dtype=np.float32)
    inv_freq = (1.0 / (np.float32(10000.0) ** (np.arange(0, 64, 2, dtype=np.float32) / np.float32(64)))).astype(np.float32)
    ang = (pos[:, None] * inv_freq[None, :]).astype(np.float32)
    cos = np.cos(ang).astype(np.float32)
    sin = np.sin(ang).astype(np.float32)
    cosT = np.concatenate([cos, cos], axis=1).T.copy()
    sinT = np.concatenate([-sin, sin], axis=1).T.copy()
    return cosT, sinT


def kernel(x, attn_norm_g, w_in, q_a_norm_g, kv_a_norm_g, w_q_b, w_kv_b, sc_conv_w,
           out_norm_attn_g, out_norm_conv_g, w_o, ffn_norm_g, w_ffn_up, ffn_conv_w,
           ffn_conv_b, w_ffn_down, final_norm_g):
    f = lambda a: np.ascontiguousarray(np.asarray(a, dtype=np.float32))
    x = f(x)
    if 'nc' not in _CACHE:
        _CACHE['nc'] = build()[0]
    nc = _CACHE['nc']
    cosT, sinT = _rope_tables()
    ctab = np.zeros((128, NCT), np.float32)
    ctab[:, C_GQ:C_GQ + 4] = f(q_a_norm_g)[0].reshape(4, 128).T
    ctab[:, C_GKV:C_GKV + 4] = f(kv_a_norm_g)[0].reshape(4, 128).T
    ctab[:, C_GA:C_GA + 8] = f(out_norm_attn_g)[0].reshape(8, 128).T
    ctab[:, C_GB:C_GB + 8] = f(out_norm_conv_g)[0].reshape(8, 128).T
    ctab[:, C_SCW:C_SCW + 24] = f(sc_conv_w)[0].reshape(3, 8, 128).transpose(2, 0, 1).reshape(128, 24)
    ctab[:, C_FCW:C_FCW + 264] = f(ffn_conv_w)[0].reshape(3, 88, 128).transpose(2, 0, 1).reshape(128, 264)
    ctab[:, C_FCB:C_FCB + 88] = f(ffn_conv_b)[0].reshape(88, 128).T
    gbc3 = np.stack([np.broadcast_to(f(attn_norm_g)[0], (128, D)),
                     np.broadcast_to(f(ffn_norm_g)[0], (128, D)),
                     np.broadcast_to(f(final_norm_g), (128, D))]).astype(np.float32).copy()
    shared = {
        "kcos": np.concatenate([cosT, cosT], axis=0), "ksin": np.concatenate([sinT, sinT], axis=0),
        "w_in": f(w_in)[0], "w_qb": f(w_q_b)[0], "w_kvb": f(w_kv_b)[0], "w_o": f(w_o)[0],
        "w_up": f(w_ffn_up)[0], "w_dn": f(w_ffn_down)[0], "gbc3": gbc3, "ctab": ctab,
        "ident": np.eye(128, dtype=np.float32),
    }
    in_maps = []
    for c in range(8):
        b, half = c // 2, c % 2
        xs = x[b]
        xext = np.zeros((2, E, D), np.float32)
        emask = np.ones((2, 128, 9), np.float32)
        qt = np.zeros((2, 128, E), np.float32)
        for p in range(2):
            a = half * 2048 + p * NOWN
            pos = np.concatenate([np.arange(a, a + NOWN), [a - 2, a - 1, a + NOWN, a + NOWN + 1]])
            ok = (pos >= 0) & (pos < SEQ)
            pc = np.clip(pos, 0, SEQ - 1)
            xext[p][ok] = xs[pos[ok]]
            qt[p, 0:64] = cosT[:, pc]
            qt[p, 64:128] = sinT[:, pc]
            emask[p, 0:4, 8] = ok[NOWN:].astype(np.float32)
        m = dict(shared)
        m.update({"xseq": xs, "xext": xext, "emask": emask, "qtab": qt})
        in_maps.append(m)
    res = run_bass_kernel_spmd(nc, in_maps, core_ids=list(range(8)))
    outp = np.empty((4, SEQ, D), np.float32)
    for c in range(8):
        b, half = c // 2, c % 2
        o = np.asarray(res.results[c]["out"]).reshape(2 * NOWN, D)
        outp[b, half * 2048:(half + 1) * 2048] = o
    return outp
```

```python
import math
import numpy as np
import concourse.bass as bass
import concourse.mybir as mybir
from concourse.bass_utils import run_bass_kernel_spmd

F32 = mybir.dt.float32
BF16 = mybir.dt.bfloat16
AF = mybir.ActivationFunctionType
ALU = mybir.AluOpType

EPS = 1e-6
D = 2048
KC = 16
SEQ = 4096
NOWN = 1024
E = 1028
GROUPS = [(0, 343), (343, 343), (686, 342)]
TILES = [(128 * t, 128) for t in range(8)] + [(1024, 4)]
NH = 8
DFF = 5632
NSUP = 22
SCALE = 1.0 / math.sqrt(192.0)

C_GQ, C_GKV, C_GA, C_GB, C_SCW, C_FCW, C_FCB, NCT = 0, 4, 8, 16, 24, 48, 312, 400

import os
PIPE = os.environ.get('K_PIPE', '1') == '1'
ZPAD = os.environ.get('K_ZPAD', '1') == '1'
SAME_SYNC = {'pe': False, 'act': True, 'dve': True, 'pool': True, 'sp': False}


def tiles_of(c0, n):
    return [t for t, (s, r) in enumerate(TILES) if s < c0 + n and s + r > c0]


class Prog:
    def __init__(self, nc):
        self.nc = nc
        self.engs = {'pe': nc.tensor, 'act': nc.scalar, 'dve': nc.vector,
                     'pool': nc.gpsimd, 'sp': nc.sync}
        self.streams = {e: [] for e in self.engs}
        self.sems = {}
        self.cnt = {}
        self.waited = {e: {} for e in self.engs}
        self.lastw = {}
        self.readers = {}
        self.fam_live = {}
        self.dead = []
        self.inherit = {}
        self.noalias = {'ps'}

    @staticmethod
    def fam_of(key):
        return key if isinstance(key, str) else key[0]

    def on_alloc(self, fam, lo, hi):
        inh = self.inherit.setdefault(fam, {})

        def merge(k, v):
            if inh.get(k, 0) < v:
                inh[k] = v
        hit = [d for d in self.dead if d[0] < hi and lo < d[1]]
        if hit:
            fams = {d[2] for d in hit}
            for k, tok in self.lastw.items():
                if self.fam_of(k) in fams:
                    merge(*tok)
            for k, rd in self.readers.items():
                if self.fam_of(k) in fams:
                    for sk, v in rd.items():
                        merge(sk, v)
            for f in fams:
                for sk, v in list(self.inherit.get(f, {}).items()):
                    merge(sk, v)
            self.dead = [d for d in self.dead if not (lo <= d[0] and d[1] <= hi)]
        self.fam_live[fam] = self.fam_live.get(fam, 0) + 1

    def on_release(self, entries):
        self.dead.extend(entries)

    def _sem(self, key):
        if key not in self.sems:
            self.sems[key] = self.nc.alloc_semaphore("s%d" % len(self.sems))
            self.cnt[key] = 0
        return self.sems[key]

    def _deps(self, eng, reads, writes):
        deps = {}

        def add(k, v):
            if deps.get(k, 0) < v:
                deps[k] = v
        for r in reads:
            if r in self.lastw:
                add(*self.lastw[r])
        for w in writes:
            if w in self.lastw:
                add(*self.lastw[w])
            for k, v in self.readers.get(w, {}).items():
                add(k, v)
        for key in list(reads) + list(writes):
            fam = self.fam_of(key)
            assert fam in self.fam_live or fam in self.noalias, ("unbound region family", key)
            for k, v in self.inherit.get(fam, {}).items():
                add(k, v)
        waits = []
        for k, v in deps.items():
            if k == eng and not SAME_SYNC[eng]:
                continue
            if self.waited[eng].get(k, 0) >= v:
                continue
            self.waited[eng][k] = v
            waits.append((k, v))
        return waits

    def _commit(self, tok, reads, writes):
        for r in reads:
            d = self.readers.setdefault(r, {})
            if d.get(tok[0], 0) < tok[1]:
                d[tok[0]] = tok[1]
        for w in writes:
            self.lastw[w] = tok
            self.readers[w] = {}

    def op(self, eng, fn, reads=(), writes=(), signal=True):
        waits = self._deps(eng, reads, writes)
        self._sem(eng)
        if signal:
            self.cnt[eng] += 1
            tok = (eng, self.cnt[eng])
        else:
            tok = (eng, self.cnt[eng] + 1)
        self.streams[eng].append((waits, fn, (eng, 1) if signal else None))
        self._commit(tok, reads, writes)

    def dma(self, q, out, in_, semkey, reads=(), writes=()):
        waits = self._deps(q, reads, writes)
        self._sem(semkey)
        self.cnt[semkey] += 16
        tok = (semkey, self.cnt[semkey])
        self.streams[q].append((waits, lambda e: e.dma_start(out=out, in_=in_), (semkey, 16)))
        self._commit(tok, reads, writes)

    def barrier(self):
        toks = [(k, c) for k, c in self.cnt.items() if c > 0]
        for e in self.engs:
            waits = []
            for k, v in toks:
                if k == e and not SAME_SYNC[e]:
                    continue
                if self.waited[e].get(k, 0) >= v:
                    continue
                self.waited[e][k] = v
                waits.append((k, v))
            if waits:
                self.streams[e].append((waits, None, None))

    def wait_all(self, eng, keys):
        waits = []
        for k in keys:
            if k in self.cnt and self.waited[eng].get(k, 0) < self.cnt[k]:
                self.waited[eng][k] = self.cnt[k]
                waits.append((k, self.cnt[k]))
        self.streams[eng].append((waits, None, None))

    def mm(self, out, lhsT, rhs, start, stop, reads, writes, signal):
        self.op('pe', lambda e: e.matmul(out, lhsT, rhs, start=start, stop=stop),
                reads, writes, signal)

    def transpose(self, out, in_, ident, reads, writes, signal):
        self.op('pe', lambda e: e.transpose(out, in_, ident), reads, writes, signal)

    def act(self, out, in_, func, reads, writes, **kw):
        self.op('act', lambda e: e.activation(out, in_, func, **kw), reads, writes)

    def copy(self, eng, out, in_, reads, writes):
        if eng == 'act':
            self.op('act', lambda e: e.copy(out, in_), reads, writes)
        else:
            self.op(eng, lambda e: e.tensor_copy(out, in_), reads, writes)

    def tt(self, out, in0, in1, op, reads, writes, eng='dve'):
        self.op(eng, lambda e: e.tensor_tensor(out, in0, in1, op), reads, writes)

    def ts(self, out, in0, s1, s2, op0, op1, reads, writes, eng='dve'):
        self.op(eng, lambda e: e.tensor_scalar(out, in0, s1, s2, op0, op1), reads, writes)

    def stt(self, out, in0, scalar, in1, op0, op1, reads, writes):
        self.op('dve', lambda e: e.scalar_tensor_tensor(out, in0, scalar, in1, op0, op1),
                reads, writes)

    def recip(self, out, in_, reads, writes):
        self.op('dve', lambda e: e.reciprocal(out, in_), reads, writes)

    def memset(self, eng, ap, val, writes):
        self.op(eng, lambda e: e.memset(ap, val), (), writes)

    def finalize(self):
        nc = self.nc
        with nc.Block() as block:
            def mk(name):
                def f(e):
                    for waits, fn, sig in self.streams[name]:
                        for k, v in waits:
                            e.wait_ge(self.sems[k], v)
                        if fn is None:
                            continue
                        inst = fn(e)
                        if sig is not None:
                            inst.then_inc(self.sems[sig[0]], sig[1])
                return f
            block.tensor(mk('pe'))
            block.scalar(mk('act'))
            block.vector(mk('dve'))
            block.gpsimd(mk('pool'))
            block.sync(mk('sp'))


class Arena:
    def __init__(self, nc, base, top, prog=None):
        self.nc, self.base, self.top, self.cur = nc, base, top, base
        self.peak = base
        self.prog = prog
        self.entries = []

    def alloc(self, name, shape, dt, fam=None):
        esz = 2 if dt == BF16 else 4
        n = esz
        for s_ in shape[1:]:
            n *= s_
        off = (self.cur + 31) // 32 * 32
        self.cur = off + n
        self.peak = max(self.peak, self.cur)
        assert self.cur <= self.top, "SBUF overflow at %s: %d > %d" % (name, self.cur, self.top)
        fam = fam or (name if name in ('kt1', 'kt2', 'sq2', 'x1') else name.rstrip('0123456789'))
        self.entries.append((off, off + n, fam))
        if self.prog is not None:
            self.prog.on_alloc(fam, off, off + n)
        return self.nc.alloc_sbuf_tensor_at(name, shape, dt, offset=off)

    def mark(self):
        return self.cur

    def skip(self, nbytes):
        self.cur += nbytes
        assert self.cur <= self.top

    def release(self, m):
        dead = [e for e in self.entries if e[0] >= m]
        self.entries = [e for e in self.entries if e[0] < m]
        if self.prog is not None:
            self.prog.on_release(dead)
        self.cur = m

    def kill(self, lo, hi):
        dead = [e for e in self.entries if lo <= e[0] and e[1] <= hi]
        self.entries = [e for e in self.entries if not (lo <= e[0] and e[1] <= hi)]
        if self.prog is not None:
            self.prog.on_release(dead)


def build(npass=2, stop_after=None):
    nc = bass.Bass("TRN2", target_bir_lowering=False)

    def din(name, shape):
        return nc.dram_tensor(name, shape, F32, kind="ExternalInput").ap()

    xseq = din("xseq", [SEQ, D])
    xext = din("xext", [2, E, D])
    emask = din("emask", [2, 128, 9])
    qtab = din("qtab", [2, 128, E])
    kcos = din("kcos", [128, SEQ])
    ksin = din("ksin", [128, SEQ])
    w_in = din("w_in", [D, 4160])
    w_qb = din("w_qb", [512, 1536])
    w_kvb = din("w_kvb", [512, 2048])
    w_o = din("w_o", [D, D])
    w_up = din("w_up", [D, 2 * DFF])
    w_dn = din("w_dn", [DFF, D])
    gbc3 = din("gbc3", [3, 128, D])
    ctab_d = din("ctab", [128, NCT])
    ident_d = din("ident", [128, 128])
    out = nc.dram_tensor("out", [2, NOWN, D], F32, kind="ExternalOutput").ap()

    P = Prog(nc)
    A = Arena(nc, (nc.sbuf_base + 63) // 64 * 64, nc.sbuf_top, P)
    psall = nc.alloc_psum_tensor("psall", [128, 8, 512], F32)
    ps = [psall[:, i, :] for i in range(8)]
    psbf = psall[:, :, :].bitcast(BF16)
    bank_rr = [0]

    def nextbank():
        b = bank_rr[0]
        bank_rr[0] = (b + 1) % 8
        return b

    ident = A.alloc("ident", [128, 128], F32)
    ones_bf = A.alloc("ones_bf", [128, 128], BF16, fam='ones')
    ctab = A.alloc("ctab", [128, NCT], F32)
    gbc = A.alloc("gbc", [128, D], F32)
    stat = A.alloc("stat", [128, 64], F32, fam='st')
    msk = A.alloc("msk", [128, 9], F32)
    P.dma('sp', ident[:, :], ident_d, 'ident', (), ['ident'])
    P.dma('sp', ctab[:, :], ctab_d, 'ctab', (), ['ctab'])
    P.memset('dve', ones_bf[:, :], 1.0, ['ones'])
    ident_bf = A.alloc("ident_bf", [128, 128], BF16, fam='identbf')
    P.copy('dve', ident_bf[:, :], ident[:, :], ['ident'], ['identbf'])
    ones_f = A.alloc("ones_f", [128, 128], F32, fam='onesf')
    P.memset('dve', ones_f[:, :], 1.0, ['onesf'])

    w_in_r = w_in.rearrange("(kc p) n -> p kc n", p=128)
    w_qb_r = w_qb.rearrange("(kc p) n -> p kc n", p=128)
    w_kvb_r = w_kvb.rearrange("(kc p) n -> p kc n", p=128)
    w_o_r = w_o.rearrange("(kc p) n -> p kc n", p=128)
    w_up_r = w_up.rearrange("(kc p) n -> p kc n", p=128)
    w_dn_r = w_dn.rearrange("(c p) n -> p c n", p=128)

    stat_rr = [0]

    def stat3():
        i = stat_rr[0]
        stat_rr[0] = (i + 1) % 16
        return i

    def load_gbc(which):
        P.dma('sp', gbc[:, :], gbc3[which], 'gbc', (), ['gbc'])

    def stage_a_head(src_ap, src_keys, rows, hs_bufs, hs_i, maskcol=None):
        si = stat3()
        ssq = stat[:rows, 4 * si:4 * si + 1]
        rs = stat[:rows, 4 * si + 1:4 * si + 2]
        rstd = stat[:rows, 4 * si + 2:4 * si + 3]
        sk = ('st', si)
        hs = hs_bufs[hs_i]
        hk = ('hs', hs_i)
        P.act(hs[:rows, :], src_ap, AF.Square, src_keys, [sk, hk], accum_out=ssq)
        P.act(rs, ssq, AF.Sqrt, [sk], [sk], scale=1.0 / D, bias=EPS)
        P.recip(rstd, rs, [sk], [sk])
        if maskcol is not None:
            P.tt(rstd, rstd, msk[:rows, maskcol:maskcol + 1], ALU.mult, [sk, 'msk'], [sk])
        P.stt(hs[:rows, :], src_ap, rstd, gbc[:rows, :], ALU.mult, ALU.mult,
              list(src_keys) + [sk, 'gbc'], [hk])
        return hs, hk

    def stage_a_tail(hs, hk, rows, dst_fn, dst_key):
        for q in range(4):
            b = nextbank()
            for j in range(4):
                kc = 4 * q + j
                P.transpose(psbf[:, b, j * 128:j * 128 + rows], hs[:rows, kc * 128:(kc + 1) * 128],
                            ident_bf[:rows, :rows], [hk, 'identbf'], [('ps', b)], signal=(j == 3))
            src = psbf[:, b, 0:512].rearrange("p (j c) -> p j c", j=4)[:, :, 0:rows]
            P.copy('dve' if q == 3 else 'act', dst_fn(4 * q), src, [('ps', b)], [dst_key])

    def run_tiles(specs, hs_bufs, after_tail=None):
        nb = len(hs_bufs)
        heads = {}

        def do_head(i):
            sp = specs[i]
            if sp.get('load') is not None:
                sp['load']()
            heads[i] = stage_a_head(sp['src'], sp['keys'], sp['rows'], hs_bufs, i % nb, sp.get('maskcol'))
        ahead = nb - 1
        for i in range(min(ahead, len(specs))):
            do_head(i)
        for i, sp in enumerate(specs):
            if i + ahead < len(specs):
                do_head(i + ahead)
            hs, hk = heads.pop(i)
            stage_a_tail(hs, hk, sp['rows'], sp['dst_fn'], sp['dst_key'])
            if after_tail is not None:
                after_tail(i)

    pass_base = A.mark()
    for ps_i in range(npass):
        A.release(pass_base)
        R1SIZE = 41472
        pb64 = (pass_base + 63) // 64 * 64
        AR1 = Arena(nc, pb64, pb64 + R1SIZE, P)
        A.cur = pb64
        A.skip(R1SIZE)
        r1_end = A.mark()
        ckvnT = A.alloc("ckvnT", [128, 4, SEQ], BF16)
        kropeT = A.alloc("kropeT", [128, SEQ], BF16)
        cqnT = A.alloc("cqnT", [128, 4, E], BF16)
        r2_end = A.mark()

        P.dma('sp', msk[:, :], emask[ps_i], 'msk', (), ['msk'])

        wkv = AR1.alloc("wkv", [128, KC, 768], BF16)
        xt = [A.alloc("xt%d" % i, [128, D], F32) for i in range(3)]
        hsb = [A.alloc("hs%d" % i, [128, D], BF16) for i in range(3)]
        junk = None
        hTkv = [A.alloc("hTkv%d" % i, [128, KC, 512], BF16) for i in range(2)]
        ckvf = A.alloc("ckvf", [128, 4, 512], F32)
        sqb = A.alloc("sqb", [128, 4, 512], BF16)
        rsb = A.alloc("rsb", [128, 512], F32)
        rstdb = A.alloc("rstdb", [128, 512], F32)
        kcs = [A.alloc("kcs%d" % i, [128, 512], F32) for i in range(1)]
        ksn = [A.alloc("ksn%d" % i, [128, 512], F32) for i in range(1)]
        kt1 = A.alloc("kt1", [128, 512], F32)
        kt2 = A.alloc("kt2", [128, 512], F32)

        load_gbc(0)
        P.dma('pool', wkv[:, :, 0:576], w_in_r[:, :, 512:1088], 'wkv', (), ['wkv'])
        P.dma('pool', wkv[:, :, 576:640], w_in_r[:, :, 1024:1088], 'wkv', (), ['wkv'])
        P.copy('dve', wkv[:, :, 640:672], wkv[:, :, 544:576], ['wkv'], ['wkv'])
        P.copy('dve', wkv[:, :, 672:704], wkv[:, :, 512:544], ['wkv'], ['wkv'])
        P.copy('dve', wkv[:, :, 704:768], wkv[:, :, 640:704], ['wkv'], ['wkv'])

        def kv_units(kb):
            hb = kb % 2
            st = {}
            us = []
            for m in range(4):
                def pe(m=m):
                    b = nextbank()
                    st[m] = b
                    for k in range(KC):
                        P.mm(ps[b][:, :], wkv[:, k, m * 128:(m + 1) * 128], hTkv[hb][:, k, :],
                             k == 0, k == KC - 1, ['wkv', ('hTkv', hb)], [('ps', b)], k == KC - 1)

                def ev(m=m):
                    b = st[m]
                    P.copy('act', ckvf[:, m, :], ps[b][:, :], [('ps', b)], [('ckvf', m)])
                    P.act(sqb[:, m, :], ps[b][:, :], AF.Square, [('ps', b)], [('sqb', m)])
                us.append((pe, ev))

            def pe_n():
                b = nextbank()
                st['n'] = b
                for m in range(4):
                    P.mm(ps[b][:, :], ones_bf[:, :], sqb[:, m, :], m == 0, m == 3,
                         ['ones', ('sqb', m)], [('ps', b)], m == 3)

            def ev_n():
                b = st['n']
                P.act(rsb[:, :], ps[b][:, :], AF.Sqrt, [('ps', b)], ['rsb'], scale=1.0 / 512, bias=EPS)
                P.recip(rstdb[:, :], rsb[:, :], ['rsb'], ['rstdb'])
                for m in range(4):
                    P.stt(ckvnT[:, m, kb * 512:(kb + 1) * 512], ckvf[:, m, :],
                          ctab[:, C_GKV + m:C_GKV + m + 1], rstdb[:, :], ALU.mult, ALU.mult,
                          [('ckvf', m), 'ctab', 'rstdb'], [('ckvnT', kb)])

            def pe_r1():
                P.dma('sp', kcs[0][:, :], kcos[:, kb * 512:(kb + 1) * 512], ('kcs', 0), (), [('kcs', 0)])
                P.dma('sp', ksn[0][:, :], ksin[:, kb * 512:(kb + 1) * 512], ('ksn', 0), (), [('ksn', 0)])
                b1 = nextbank()
                st['r1'] = b1
                for k in range(KC):
                    P.mm(ps[b1][:, :], wkv[:, k, 512:640], hTkv[hb][:, k, :], k == 0, k == KC - 1,
                         ['wkv', ('hTkv', hb)], [('ps', b1)], k == KC - 1)

            def ev_r1():
                b1 = st['r1']
                P.tt(kt1[:, :], ps[b1][:, :], kcs[0][:, :], ALU.mult, [('ps', b1), ('kcs', 0)], ['kt1'])

            def pe_r2():
                b2 = nextbank()
                st['r2'] = b2
                for k in range(KC):
                    P.mm(ps[b2][:, :], wkv[:, k, 640:768], hTkv[hb][:, k, :], k == 0, k == KC - 1,
                         ['wkv', ('hTkv', hb)], [('ps', b2)], k == KC - 1)

            def ev_r2():
                b2 = st['r2']
                P.tt(kt2[:, :], ps[b2][:, :], ksn[0][:, :], ALU.mult, [('ps', b2), ('ksn', 0)], ['kt2'])
                P.tt(kropeT[:, kb * 512:(kb + 1) * 512], kt1[:, :], kt2[:, :], ALU.add,
                     ['kt1', 'kt2'], [('kropeT', kb)])
            us2 = us + [(pe_r1, ev_r1), (pe_r2, ev_r2), (pe_n, ev_n)]
            return us2

        uq = []
        evq = []

        def pump(nunits):
            while evq:
                evq.pop(0)()
            for _ in range(nunits):
                if uq:
                    pe, ev = uq.pop(0)
                    pe()
                    evq.append(ev)

        def after_tail(i):
            if i % 4 == 3:
                uq.extend(kv_units(i // 4))
            pump(2)

        specs = []
        for i in range(32):
            kb, t = i // 4, i % 4
            xi = i % 3

            def ld(i=i, xi=xi):
                P.dma('sp', xt[xi][:, :], xseq[i * 128:(i + 1) * 128, :], ('xt', xi), (), [('xt', xi)])
            specs.append(dict(load=ld, src=xt[xi][:, :], keys=[('xt', xi)], rows=128,
                              dst_fn=(lambda kc0, hb=kb % 2, t=t: hTkv[hb][:, kc0:kc0 + 4, t * 128:(t + 1) * 128]),
                              dst_key=('hTkv', kb % 2)))
        run_tiles(specs, hsb, after_tail=after_tail)
        while uq or evq:
            pump(1)
        A.release(r2_end)
        AR1.release(AR1.base)
        ycT = AR1.alloc("ycT", [128, 8, E], BF16)
        aT = AR1.alloc("aT", [128, 8, E], BF16)
        sqaccA = AR1.alloc("sqaccA", [128, E], F32)
        sqaccB = AR1.alloc("sqaccB", [128, E], F32)
        if stop_after == 'P0':
            break

        hT = A.alloc("hT", [128, KC, E], BF16)
        p1_mark = A.mark()
        xt = [A.alloc("xt%d" % i, [128, D], F32) for i in range(3)]
        hsb = [A.alloc("hs%d" % i, [128, D], BF16) for i in range(3)]
        junk = None
        specs = []
        for t, (c0, rows) in enumerate(TILES):
            xi = t % 3

            def ld(t=t, xi=xi, c0=c0, rows=rows):
                P.dma('sp', xt[xi][:rows, :], xext[ps_i, c0:c0 + rows, :], ('xt', xi), (), [('xt', xi)])
            specs.append(dict(load=ld, src=xt[xi][:rows, :], keys=[('xt', xi)], rows=rows,
                              dst_fn=(lambda kc0, c0=c0, rows=rows: hT[:, kc0:kc0 + 4, c0:c0 + rows]),
                              dst_key=('hT', t)))
        run_tiles(specs, hsb)
        A.release(p1_mark)
        cqf = A.alloc("cqf", [128, 4, E], F32)
        sqq = A.alloc("sqq", [128, 4, E], BF16)
        win = [A.alloc("win%d" % i, [128, KC, 128], BF16) for i in range(3)]
        Cf = A.alloc("Cf", [128, E], F32)
        CHc = A.alloc("CHc", [128, E], F32)
        Bc = A.alloc("Bc", [128, E], F32)
        yf = A.alloc("yf", [128, 1026], F32)
        sqt = A.alloc("sqt", [128, 1026], F32)
        rsq = A.alloc("rsq", [128, 343], F32)
        rstdq = A.alloc("rstdq", [128, 343], F32)

        wcnt = [0]

        def in_chunk(col0, evac):
            wi = wcnt[0] % 3
            wcnt[0] += 1
            P.dma('pool', win[wi][:, :, :], w_in_r[:, :, col0:col0 + 128], ('win', wi), (), [('win', wi)])
            for g, (c0, n) in enumerate(GROUPS):
                b = nextbank()
                rk = [('win', wi)] + [('hT', t) for t in tiles_of(c0, n)]
                for k in range(KC):
                    P.mm(ps[b][:, 0:n], win[wi][:, k, :], hT[:, k, c0:c0 + n], k == 0, k == KC - 1,
                         rk, [('ps', b)], k == KC - 1)
                evac(g, c0, n, b)

        for m in range(4):
            def ev(g, c0, n, b, m=m):
                P.copy('act', cqf[:, m, c0:c0 + n], ps[b][:, 0:n], [('ps', b)], [('cqf', m, g)])
                P.act(sqq[:, m, c0:c0 + n], ps[b][:, 0:n], AF.Square, [('ps', b)], [('sqq', m, g)])
            in_chunk(m * 128, ev)
        for g, (c0, n) in enumerate(GROUPS):
            b = nextbank()
            for m in range(4):
                P.mm(ps[b][:, 0:n], ones_bf[:, :], sqq[:, m, c0:c0 + n], m == 0, m == 3,
                     ['ones', ('sqq', m, g)], [('ps', b)], m == 3)
            P.act(rsq[:, 0:n], ps[b][:, 0:n], AF.Sqrt, [('ps', b)], ['rsq'], scale=1.0 / 512, bias=EPS)
            P.recip(rstdq[:, 0:n], rsq[:, 0:n], ['rsq'], ['rstdq'])
            for m in range(4):
                P.stt(cqnT[:, m, c0:c0 + n], cqf[:, m, c0:c0 + n], ctab[:, C_GQ + m:C_GQ + m + 1],
                      rstdq[:, 0:n], ALU.mult, ALU.mult, [('cqf', m, g), 'ctab', 'rstdq'], [('cqnT', g)])

        def ext_to_conv(dst, src_fn, g, c0, n, b, eng, wkey, extra_reads=()):
            hi = min(c0 + n, 1024)
            src_fn(dst[:, 2 + c0:2 + hi], c0, hi, 0)
            if c0 + n > 1024:
                src_fn(dst[:, 0:2], 1024, 1026, 1)
                src_fn(dst[:, 1026:1028], 1026, 1028, 2)

        P.memset('dve', ycT[:, :, 1024:1025], 0.0, [('ycT', 'pad')])
        P.memset('dve', ycT[:, :, 1027:1028], 0.0, [('ycT', 'pad')])
        for c in range(8):
            def evC(g, c0, n, b):
                P.copy('act', Cf[:, c0:c0 + n], ps[b][:, 0:n], [('ps', b)], [('Cf', g)])
            in_chunk(1088 + 1024 + c * 128, evC)

            def evH(g, c0, n, b):
                def f(dst, lo, hi, part):
                    P.tt(dst, ps[b][:, lo - c0:hi - c0], Cf[:, lo:hi], ALU.mult,
                         [('ps', b), ('Cf', g)], [('CHc', g, part)])
                ext_to_conv(CHc, f, g, c0, n, b, 'dve', 'CHc')
            in_chunk(1088 + 2048 + c * 128, evH)

            def evB(g, c0, n, b):
                def f(dst, lo, hi, part):
                    P.copy('act', dst, ps[b][:, lo - c0:hi - c0], [('ps', b)], [('Bc', g, part)])
                ext_to_conv(Bc, f, g, c0, n, b, 'act', 'Bc')
            in_chunk(1088 + c * 128, evB)

            chk = [('CHc', g, p) for g in range(3) for p in range(3 if g == 2 else 1)]
            bck = [('Bc', g, p) for g in range(3) for p in range(3 if g == 2 else 1)]
            w0 = ctab[:, C_SCW + c:C_SCW + c + 1]
            w1 = ctab[:, C_SCW + 8 + c:C_SCW + 8 + c + 1]
            w2 = ctab[:, C_SCW + 16 + c:C_SCW + 16 + c + 1]
            P.ts(yf[:, :], CHc[:, 1:1027], w1, None, ALU.mult, ALU.bypass, chk + ['ctab'], ['yf'])
            P.stt(yf[:, :], CHc[:, 0:1026], w0, yf[:, :], ALU.mult, ALU.add, chk + ['ctab', 'yf'], ['yf'])
            P.stt(yf[:, :], CHc[:, 2:1028], w2, yf[:, :], ALU.mult, ALU.add, chk + ['ctab', 'yf'], ['yf'])
            P.tt(yf[:, :], yf[:, :], Bc[:, 1:1027], ALU.mult, ['yf'] + bck, ['yf'])
            gB = ctab[:, C_GB + c:C_GB + c + 1]
            P.ts(ycT[:, c, 0:1024], yf[:, 1:1025], gB, None, ALU.mult, ALU.bypass,
                 ['yf', 'ctab'], [('ycT', c)])
            P.ts(ycT[:, c, 1025:1027], yf[:, 0:1026:1025], gB, None, ALU.mult, ALU.bypass,
                 ['yf', 'ctab'], [('ycT', c)])
            if c == 0:
                P.memset('dve', sqaccB[:, 1024:1028], 0.0, ['sqaccB'])
                P.tt(sqaccB[:, 0:1024], yf[:, 1:1025], yf[:, 1:1025], ALU.mult, ['yf'], ['sqaccB'])
                P.tt(sqaccB[:, 1025:1027], yf[:, 0:1026:1025], yf[:, 0:1026:1025], ALU.mult,
                     ['yf'], ['sqaccB'])
            else:
                P.tt(sqt[:, :], yf[:, :], yf[:, :], ALU.mult, ['yf'], ['sqt'])
                P.tt(sqaccB[:, 0:1024], sqaccB[:, 0:1024], sqt[:, 1:1025], ALU.add,
                     ['sqt', 'sqaccB'], ['sqaccB'])
                P.tt(sqaccB[:, 1025:1027], sqaccB[:, 1025:1027], sqt[:, 0:1026:1025], ALU.add,
                     ['sqt', 'sqaccB'], ['sqaccB'])
        A.release(r2_end)
        if stop_after == 'P1':
            break

        KhT = [A.alloc("KhT%d" % i, [128, SEQ], BF16) for i in range(2)]
        Vh = [A.alloc("Vh%d" % i, [128, 32, 128], BF16) for i in range(2)]
        A.alloc("padP2", [128, 64], BF16)
        wkvb = A.alloc("wkvb", [128, 4, 2048], BF16)
        wqb = A.alloc("wqb", [128, 4, 1536], BF16)
        wqr = A.alloc("wqr", [128, 4, NH, 128], BF16)
        QhN = [A.alloc("QhN%d" % i, [128, E], BF16) for i in range(2)]
        QhR = [A.alloc("QhR%d" % i, [128, E], BF16) for i in range(2)]
        NPT = 8
        PT = [A.alloc("PT%d" % i, [128, 2, 344], BF16) for i in range(NPT)]
        accD = A.alloc("accD", [128, 2, 344], F32)
        accP = A.alloc("accP", [128, 2, 344], F32)
        tsum = [A.alloc("tsum%d" % i, [128, 344], F32) for i in range(2)]
        rec = A.alloc("rec", [128, 344], F32)
        attn = A.alloc("attn", [128, 344], F32)
        sq2 = A.alloc("sq2", [128, 344], F32)
        qT = A.alloc("qT", [128, E], F32)

        P.dma('pool', wkvb[:, :, :], w_kvb_r, 'wkvb', (), ['wkvb'])
        P.dma('pool', wqb[:, :, :], w_qb_r, 'wqb', (), ['wqb'])
        wqbv = wqb[:, :, :].rearrange("p k (h d) -> p k h d", d=192)
        P.copy('dve', wqr[:, :, :, 0:64], wqbv[:, :, :, 128:192], ['wqb'], ['wqr'])
        P.copy('dve', wqr[:, :, :, 64:96], wqbv[:, :, :, 160:192], ['wqb'], ['wqr'])
        P.copy('dve', wqr[:, :, :, 96:128], wqbv[:, :, :, 128:160], ['wqb'], ['wqr'])
        P.dma('sp', qT[:, :], qtab[ps_i], 'qT', (), ['qT'])

        SPAIR, OB, MB = [0, 2], [4, 5], [6, 7]
        mrr = [0]

        def miscbank():
            b = MB[mrr[0] % 2]
            mrr[0] += 1
            return b

        def expansion_units(h):
            hb = h % 2
            us = []
            for kb in range(8):
                def uK(kb=kb):
                    b = miscbank()
                    for k in range(4):
                        P.mm(ps[b][:, :], wkvb[:, k, h * 256:h * 256 + 128],
                             ckvnT[:, k, kb * 512:(kb + 1) * 512], k == 0, k == 3,
                             ['wkvb', ('ckvnT', kb)], [('ps', b)], k == 3)
                    P.copy('dve', KhT[hb][:, kb * 512:(kb + 1) * 512], ps[b][:, :], [('ps', b)], [('KhT', hb, kb)])
                us.append(uK)

                def uV(kb=kb):
                    b = miscbank()
                    for t in range(4):
                        tok = kb * 512 + t * 128
                        for k in range(4):
                            P.mm(ps[b][:, t * 128:(t + 1) * 128], ckvnT[:, k, tok:tok + 128],
                                 wkvb[:, k, h * 256 + 128:h * 256 + 256], k == 0, k == 3,
                                 ['wkvb', ('ckvnT', kb)], [('ps', b)], (k == 3 and t == 3))
                    P.copy('dve', Vh[hb][:, kb * 4:(kb + 1) * 4, :],
                           ps[b][:, :].rearrange("p (t c) -> p t c", t=4), [('ps', b)], [('Vh', hb, kb)])
                us.append(uV)
            for g, (c0, n) in enumerate(GROUPS):
                def uQ1(g=g, c0=c0, n=n):
                    b = miscbank()
                    for k in range(4):
                        P.mm(ps[b][:, 0:n], wqbv[:, k, h, 0:128], cqnT[:, k, c0:c0 + n],
                             k == 0, k == 3, ['wqb', ('cqnT', g)], [('ps', b)], k == 3)
                    P.copy('dve', QhN[hb][:, c0:c0 + n], ps[b][:, 0:n], [('ps', b)], [('QhN', hb, g)])
                us.append(uQ1)

                def uQ2(g=g, c0=c0, n=n):
                    b1 = miscbank()
                    for k in range(4):
                        P.mm(ps[b1][:, 0:n], wqr[:, k, h, :], cqnT[:, k, c0:c0 + n],
                             k == 0, k == 3, ['wqr', ('cqnT', g)], [('ps', b1)], k == 3)
                    P.tt(QhR[hb][:, c0:c0 + n], ps[b1][:, 0:n], qT[:, c0:c0 + n], ALU.mult,
                         [('ps', b1), 'qT'], [('QhR', hb, g)])
                us.append(uQ2)
            return us

        for u in expansion_units(0):
            u()
        gcount = 0
        epi_pending = []
        for h in range(NH):
            hb = h % 2
            pend = expansion_units(h + 1) if h + 1 < NH else []
            npend = len(pend)
            nsteps = 3 * 16
            step = 0
            for g, (c0, n) in enumerate(GROUPS):
                ob = OB[gcount % 2]
                gcount += 1

                def qkpair(j):
                    a = SPAIR[j % 2]
                    for u in range(2):
                        kc = 2 * j + u
                        kb = kc // 4
                        P.mm(ps[a + u][:, 0:n], KhT[hb][:, kc * 128:(kc + 1) * 128], QhN[hb][:, c0:c0 + n],
                             True, False, [('KhT', hb, kb), ('QhN', hb, g)], [('ps', a + u)], False)
                        P.mm(ps[a + u][:, 0:n], kropeT[:, kc * 128:(kc + 1) * 128], QhR[hb][:, c0:c0 + n],
                             False, True, [('kropeT', kb), ('QhR', hb, g)],
                             [('ps', a + u)], True)
                    pi = (gcount * 16 + j) % NPT
                    P.act(PT[pi][:, :, 0:n], psall[:, a:a + 2, 0:n], AF.Exp,
                          [('ps', a), ('ps', a + 1)], [('PT', pi)], scale=SCALE)

                def pvpair(j):
                    pi = (gcount * 16 + j) % NPT
                    for u in range(2):
                        kc = 2 * j + u
                        P.mm(ps[ob][:, 0:n], Vh[hb][:, kc, :], PT[pi][:, u, 0:n], kc == 0, kc == 31,
                             [('Vh', hb, kc // 4), ('PT', pi)], [('ps', ob)], u == 1)
                    if j % 2 == 1:
                        eng, acc, nm = 'pool', accP, 'accP'
                        first = (j == 1)
                    else:
                        eng, acc, nm = 'dve', accD, 'accD'
                        first = (j == 0)
                    if first:
                        P.copy(eng, acc[:, :, 0:n], PT[pi][:, :, 0:n], [('PT', pi)], [nm])
                    else:
                        P.tt(acc[:, :, 0:n], acc[:, :, 0:n], PT[pi][:, :, 0:n], ALU.add,
                             [nm, ('PT', pi)], [nm], eng=eng)

                qkpair(0)
                for j in range(16):
                    if j + 1 < 16:
                        qkpair(j + 1)
                    pvpair(j)
                    step += 1
                    if j == 3 and epi_pending:
                        epi_pending.pop(0)()
                    while pend and (npend - len(pend)) < step * npend // nsteps:
                        pend.pop(0)()
                tsb = tsum[gcount % 2]
                tsk = ('tsum', gcount % 2)
                P.tt(tsb[:, 0:n], accD[:, 0, 0:n], accD[:, 1, 0:n], ALU.add, ['accD'], [tsk])
                P.tt(tsb[:, 0:n], tsb[:, 0:n], accP[:, 0, 0:n], ALU.add, [tsk, 'accP'], [tsk])
                P.tt(tsb[:, 0:n], tsb[:, 0:n], accP[:, 1, 0:n], ALU.add, [tsk, 'accP'], [tsk])

                def epi_b(h=h, g=g, c0=c0, n=n, ob=ob, tsb=tsb, tsk=tsk):
                    db = miscbank()
                    P.mm(ps[db][:, 0:n], ones_f[:, :], tsb[:, 0:n], True, True, ['onesf', tsk], [('ps', db)], True)
                    P.recip(rec[:, 0:n], ps[db][:, 0:n], [('ps', db)], ['rec'])
                    P.tt(attn[:, 0:n], ps[ob][:, 0:n], rec[:, 0:n], ALU.mult, [('ps', ob), 'rec'], ['attn'])
                    P.ts(aT[:, h, c0:c0 + n], attn[:, 0:n], ctab[:, C_GA + h:C_GA + h + 1], None,
                         ALU.mult, ALU.bypass, ['attn', 'ctab'], [('aT', h, g)])
                    if h == 0:
                        P.tt(sqaccA[:, c0:c0 + n], attn[:, 0:n], attn[:, 0:n], ALU.mult, ['attn'], [('sqaccA', g)])
                    else:
                        P.tt(sq2[:, 0:n], attn[:, 0:n], attn[:, 0:n], ALU.mult, ['attn'], ['sq2'])
                        P.tt(sqaccA[:, c0:c0 + n], sqaccA[:, c0:c0 + n], sq2[:, 0:n], ALU.add,
                             ['sq2', ('sqaccA', g)], [('sqaccA', g)])
                epi_pending.append(epi_b)
            while pend:
                pend.pop(0)()
        while epi_pending:
            epi_pending.pop(0)()
        A.release(r1_end)
        if stop_after == 'P2':
            break

        x1 = A.alloc("x1", [128, 8, D], F32)
        h2T = A.alloc("h2T", [128, KC, E], BF16)
        x1h = A.alloc("x1h", [128, D], F32, fam='x1')
        p3_mark = A.mark()
        wo = [A.alloc("wo%d" % i, [128, KC, 512], BF16) for i in range(2)]
        sqbA = A.alloc("sqbA", [128, E], BF16)
        sqbB = A.alloc("sqbB", [128, E], BF16)
        rstdAB = A.alloc("rstdAB", [128, 9, 4], F32, fam='rAB')

        def x1tile(t):
            return x1[:, t, :] if t < 8 else x1h[0:4, :]

        def x1_load(t, after=None):
            c0, rows = TILES[t]
            dst = x1[:rows, t, :] if t < 8 else x1h[0:4, :]
            P.dma('sp', dst, xext[ps_i, c0:c0 + rows, :], ('x1ld', t),
                  [('x1', after, 0)] if after is not None else (), [('x1', t, fb) for fb in range(4)])
        x1_load(0)
        x1_load(1)
        P.copy('dve', sqbA[:, :], sqaccA[:, :], [('sqaccA', g) for g in range(3)], ['sqbA'])
        P.copy('dve', sqbB[:, :], sqaccB[:, :], ['sqaccB'], ['sqbB'])
        for t, (c0, rows) in enumerate(TILES):
            for j, (sqx, nm) in enumerate(((sqbA, 'sqbA'), (sqbB, 'sqbB'))):
                b = nextbank()
                P.mm(ps[b][:rows, 0:1], sqx[:, c0:c0 + rows], ones_bf[:, 0:1], True, True,
                     [nm, 'ones'], [('ps', b)], True)
                P.act(rstdAB[:rows, t, 2 + j:3 + j], ps[b][:rows, 0:1], AF.Sqrt, [('ps', b)],
                      [('rAB', t, j)], scale=1.0 / 1024, bias=EPS)
                P.recip(rstdAB[:rows, t, j:j + 1], rstdAB[:rows, t, 2 + j:3 + j], [('rAB', t, j)], [('rAB', t, j)])
        for fb in range(4):
            wi = fb % 2
            P.dma('pool', wo[wi][:, :, :], w_o_r[:, :, fb * 512:(fb + 1) * 512], ('wo', wi), (), [('wo', wi)])
            for t, (c0, rows) in enumerate(TILES):
                gs = [g for g, (g0, gn) in enumerate(GROUPS) if g0 < c0 + rows and g0 + gn > c0]
                ba = nextbank()
                for hh in range(8):
                    P.mm(ps[ba][:rows, :], aT[:, hh, c0:c0 + rows], wo[wi][:, hh, :], hh == 0, hh == 7,
                         [('aT', hh, g) for g in gs] + [('wo', wi)], [('ps', ba)], hh == 7)
                bb = nextbank()
                for c in range(8):
                    P.mm(ps[bb][:rows, :], ycT[:, c, c0:c0 + rows], wo[wi][:, 8 + c, :], c == 0, c == 7,
                         [('ycT', c), ('ycT', 'pad'), ('wo', wi)], [('ps', bb)], c == 7)
                xs = (x1[:rows, t, fb * 512:(fb + 1) * 512] if t < 8 else x1h[0:4, fb * 512:(fb + 1) * 512])
                P.stt(xs, ps[ba][:rows, :], rstdAB[:rows, t, 0:1], xs, ALU.mult, ALU.add,
                      [('ps', ba), ('rAB', t, 0), ('x1', t, fb)], [('x1', t, fb)])
                P.stt(xs, ps[bb][:rows, :], rstdAB[:rows, t, 1:2], xs, ALU.mult, ALU.add,
                      [('ps', bb), ('rAB', t, 1), ('x1', t, fb)], [('x1', t, fb)])
                if fb == 0 and t + 2 < len(TILES):
                    x1_load(t + 2, after=t)
        A.release(p3_mark)
        hsb = [A.alloc("hs%d" % i, [128, D], BF16) for i in range(3)]
        junk = None
        load_gbc(1)
        specs = []
        for t, (c0, rows) in enumerate(TILES):
            src = x1[:rows, t, :] if t < 8 else x1h[0:4, :]
            specs.append(dict(load=None, src=src, keys=[('x1', t, fb) for fb in range(4)], rows=rows,
                              dst_fn=(lambda kc0, c0=c0, rows=rows: h2T[:, kc0:kc0 + 4, c0:c0 + rows]),
                              dst_key=('h2T', t), maskcol=t))
        run_tiles(specs, hsb)
        A.release(p3_mark)
        if stop_after == 'P3b':
            break

        AR1.release(AR1.base)
        wug = [AR1.alloc("wug%d" % i, [128, KC, 256], BF16) for i in range(2)]
        wuv = [AR1.alloc("wuv%d" % i, [128, KC, 256], BF16) for i in range(2)]
        ug = [A.alloc("ug%d" % i, [128, 1026], F32) for i in range(2)]
        uv = [A.alloc("uv%d" % i, [128, 1026], F32) for i in range(2)]
        wdn = [A.alloc("wdn%d" % i, [128, 2, D], BF16) for i in range(2)]
        tg = A.alloc("tg", [128, 1024], F32)
        tv = A.alloc("tv", [128, 1024], F32)
        actT = [A.alloc("actT%d" % i, [128, 2, 1024], BF16) for i in range(2)]

        h2keys = {g: [('h2T', t) for t in tiles_of(c0, n)] for g, (c0, n) in enumerate(GROUPS)}

        def up_dma(s):
            si = s % 2
            P.dma('pool', wug[si][:, :, :], w_up_r[:, :, 256 * s:256 * s + 256], ('wug', si), (), [('wug', si)])
            P.dma('pool', wuv[si][:, :, :], w_up_r[:, :, DFF + 256 * s:DFF + 256 * s + 256], ('wuv', si), (), [('wuv', si)])

        def dn_dma(s):
            si = s % 2
            P.dma('pool', wdn[si][:, :, :], w_dn_r[:, 2 * s:2 * s + 2, :], ('wdn', si), (), [('wdn', si)])

        def up(s, inter=()):
            inter = list(inter)
            left = [12]

            def spread():
                left[0] -= 1
                if left[0] >= 8:
                    return
                k = -(-len(inter) // (left[0] + 1))
                for _ in range(min(k, len(inter))):
                    inter.pop(0)()
            si = s % 2
            for jj in range(2):
                ui = (2 * s + jj) % 2
                for (wbuf, wnm, ubuf, unm) in ((wug, 'wug', ug, 'ug'), (wuv, 'wuv', uv, 'uv')):
                    for g, (c0, n) in enumerate(GROUPS):
                        b = nextbank()
                        for k in range(KC):
                            P.mm(ps[b][:, 0:n], wbuf[si][:, k, jj * 128:(jj + 1) * 128], h2T[:, k, c0:c0 + n],
                                 k == 0, k == KC - 1, [(wnm, si)] + h2keys[g], [('ps', b)], k == KC - 1)
                        hi = min(c0 + n, 1024)
                        P.copy('act', ubuf[ui][:, 1 + c0:1 + hi], ps[b][:, 0:hi - c0], [('ps', b)], [(unm, ui, g, 0)])
                        if c0 + n > 1024:
                            P.copy('act', ubuf[ui][:, 0:1026:1025], ps[b][:, 1025 - c0:1027 - c0],
                                   [('ps', b)], [(unm, ui, g, 1)])
                        spread()
                J = 2 * s + jj
                for (ubuf, unm, tbuf, tnm, Jg) in ((ug, 'ug', tg, 'tg', J), (uv, 'uv', tv, 'tv', 44 + J)):
                    uk = [(unm, ui, g, p) for g in range(3) for p in range(2 if g == 2 else 1)]
                    w0 = ctab[:, C_FCW + Jg:C_FCW + Jg + 1]
                    w1 = ctab[:, C_FCW + 88 + Jg:C_FCW + 88 + Jg + 1]
                    w2 = ctab[:, C_FCW + 176 + Jg:C_FCW + 176 + Jg + 1]
                    bb_ = ctab[:, C_FCB + Jg:C_FCB + Jg + 1]
                    P.act(tbuf[:, :], ubuf[ui][:, 1:1025], AF.Identity, uk + ['ctab'], [tnm], scale=w1, bias=bb_)
                    P.stt(tbuf[:, :], ubuf[ui][:, 0:1024], w0, tbuf[:, :], ALU.mult, ALU.add,
                          uk + ['ctab', tnm], [tnm])
                    P.stt(tbuf[:, :], ubuf[ui][:, 2:1026], w2, tbuf[:, :], ALU.mult, ALU.add,
                          uk + ['ctab', tnm], [tnm])
                P.act(tg[:, :], tg[:, :], AF.Silu, ['tg'], ['tg'])
                P.tt(actT[si][:, jj, :], tg[:, :], tv[:, :], ALU.mult, ['tg', 'tv'], [('actT', si, jj)], eng='pool')
            while inter:
                inter.pop(0)()

        def down_groups(s):
            si = s % 2
            gs = []
            for t in range(8):
                for fb in range(4):
                    def grp(t=t, fb=fb):
                        b = nextbank()
                        for jj in range(2):
                            P.mm(ps[b][:, :], actT[si][:, jj, t * 128:(t + 1) * 128],
                                 wdn[si][:, jj, fb * 512:(fb + 1) * 512], jj == 0, jj == 1,
                                 [('actT', si, jj), ('wdn', si)], [('ps', b)], jj == 1)
                        xs = x1[:, t, fb * 512:(fb + 1) * 512]
                        P.tt(xs, xs, ps[b][:, :], ALU.add, [('ps', b), ('x1', t, fb)], [('x1', t, fb)])
                    gs.append(grp)
            return gs

        up_dma(0)
        up_dma(1)
        dn_dma(0)
        up(0)
        for s in range(NSUP):
            if s + 2 < NSUP:
                up_dma(s + 2)
            if s + 1 < NSUP:
                dn_dma(s + 1)
                up(s + 1, inter=down_groups(s))
            else:
                for g_ in down_groups(s):
                    g_()
        A.release(p3_mark)
        AR1.release(AR1.base)

        ob_ = [A.alloc("ob%d" % i, [128, D], F32) for i in range(2)]
        load_gbc(2)
        for t in reversed(range(8)):
            si = stat3()
            ssq = stat[:, 4 * si:4 * si + 1]
            rs = stat[:, 4 * si + 1:4 * si + 2]
            rstd = stat[:, 4 * si + 2:4 * si + 3]
            sk = ('st', si)
            xk = [('x1', t, fb) for fb in range(4)]
            oi = t % 2
            P.act(ob_[oi][:, :], x1[:, t, :], AF.Square, xk, [sk, ('ob', oi)], accum_out=ssq)
            P.act(rs, ssq, AF.Sqrt, [sk], [sk], scale=1.0 / D, bias=EPS)
            P.recip(rstd, rs, [sk], [sk])
            P.stt(ob_[oi][:, :], x1[:, t, :], rstd, gbc[:, :], ALU.mult, ALU.mult,
                  xk + [sk, 'gbc'], [('ob', oi)])
            P.dma('sp', out[ps_i, t * 128:(t + 1) * 128, :], ob_[oi][:, :], ('ost', oi), [('ob', oi)], [('ob', oi, 'st')])
        A.release(pass_base)

    P.wait_all('sp', [k for k in P.cnt if isinstance(k, tuple) and k[0] == 'ost'])
    P.finalize()
    return nc, A.peak


_CACHE = {}


def _rope_tables():
    pos = np.arange(SEQ, dtype=np.float32)
    inv_freq = (1.0 / (np.float32(10000.0) ** (np.arange(0, 64, 2, dtype=np.float32) / np.float32(64)))).astype(np.float32)
    ang = (pos[:, None] * inv_freq[None, :]).astype(np.float32)
    cos = np.cos(ang).astype(np.float32)
    sin = np.sin(ang).astype(np.float32)
    cosT = np.concatenate([cos, cos], axis=1).T.copy()
    sinT = np.concatenate([-sin, sin], axis=1).T.copy()
    return cosT, sinT


def kernel(x, attn_norm_g, w_in, q_a_norm_g, kv_a_norm_g, w_q_b, w_kv_b, sc_conv_w,
           out_norm_attn_g, out_norm_conv_g, w_o, ffn_norm_g, w_ffn_up, ffn_conv_w,
           ffn_conv_b, w_ffn_down, final_norm_g):
    f = lambda a: np.ascontiguousarray(np.asarray(a, dtype=np.float32))
    x = f(x)
    if 'nc' not in _CACHE:
        _CACHE['nc'] = build()[0]
    nc = _CACHE['nc']
    cosT, sinT = _rope_tables()
    ctab = np.zeros((128, NCT), np.float32)
    ctab[:, C_GQ:C_GQ + 4] = f(q_a_norm_g)[0].reshape(4, 128).T
    ctab[:, C_GKV:C_GKV + 4] = f(kv_a_norm_g)[0].reshape(4, 128).T
    ctab[:, C_GA:C_GA + 8] = f(out_norm_attn_g)[0].reshape(8, 128).T
    ctab[:, C_GB:C_GB + 8] = f(out_norm_conv_g)[0].reshape(8, 128).T
    ctab[:, C_SCW:C_SCW + 24] = f(sc_conv_w)[0].reshape(3, 8, 128).transpose(2, 0, 1).reshape(128, 24)
    ctab[:, C_FCW:C_FCW + 264] = f(ffn_conv_w)[0].reshape(3, 88, 128).transpose(2, 0, 1).reshape(128, 264)
    ctab[:, C_FCB:C_FCB + 88] = f(ffn_conv_b)[0].reshape(88, 128).T
    gbc3 = np.stack([np.broadcast_to(f(attn_norm_g)[0], (128, D)),
                     np.broadcast_to(f(ffn_norm_g)[0], (128, D)),
                     np.broadcast_to(f(final_norm_g), (128, D))]).astype(np.float32).copy()
    shared = {
        "kcos": np.concatenate([cosT, cosT], axis=0), "ksin": np.concatenate([sinT, sinT], axis=0),
        "w_in": f(w_in)[0], "w_qb": f(w_q_b)[0], "w_kvb": f(w_kv_b)[0], "w_o": f(w_o)[0],
        "w_up": f(w_ffn_up)[0], "w_dn": f(w_ffn_down)[0], "gbc3": gbc3, "ctab": ctab,
        "ident": np.eye(128, dtype=np.float32),
    }
    in_maps = []
    for c in range(8):
        b, half = c // 2, c % 2
        xs = x[b]
        xext = np.zeros((2, E, D), np.float32)
        emask = np.ones((2, 128, 9), np.float32)
        qt = np.zeros((2, 128, E), np.float32)
        for p in range(2):
            a = half * 2048 + p * NOWN
            pos = np.concatenate([np.arange(a, a + NOWN), [a - 2, a - 1, a + NOWN, a + NOWN + 1]])
            ok = (pos >= 0) & (pos < SEQ)
            pc = np.clip(pos, 0, SEQ - 1)
            xext[p][ok] = xs[pos[ok]]
            qt[p, 0:64] = cosT[:, pc]
            qt[p, 64:128] = sinT[:, pc]
            emask[p, 0:4, 8] = ok[NOWN:].astype(np.float32)
        m = dict(shared)
        m.update({"xseq": xs, "xext": xext, "emask": emask, "qtab": qt})
        in_maps.append(m)
    res = run_bass_kernel_spmd(nc, in_maps, core_ids=list(range(8)))
    outp = np.empty((4, SEQ, D), np.float32)
    for c in range(8):
        b, half = c // 2, c % 2
        o = np.asarray(res.results[c]["out"]).reshape(2 * NOWN, D)
        outp[b, half * 2048:(half + 1) * 2048] = o
    return outp
```
